# Optimizing a Trainium2 kernel written in Bass

```python
import math
import jax, jax.numpy as jnp
from jax import lax
import numpy as np

D_MODEL = 1024
BATCH = 8
SEQ = 2048
DEPTH = 4

CTX_LEN = 256
GRID_W = 64

MLA_HEADS = 4
QK_NOPE = 64
QK_ROPE = 32
V_DIM = 64
Q_LORA = 256
KV_LORA = 128
ATTN_W = MLA_HEADS * V_DIM
Q_BLOCK = 128
ROPE_BASE = 10000.0
AXIS_DIM = QK_ROPE // 2
AXIS_FREQS = AXIS_DIM // 2

CONV_W = 256
CONV_K = 31

FOURIER_W = 256
FOURIER_GROUPS = 4
FOURIER_GDIM = FOURIER_W // FOURIER_GROUPS

SGU_W = 256
SGU_GROUPS = 4
SGU_GDIM = SGU_W // SGU_GROUPS
CHUNK = 128

MIX_W = ATTN_W + CONV_W + FOURIER_W + SGU_W

Q_OFF = 0
KV_OFF = Q_OFF + Q_LORA
ROPE_OFF = KV_OFF + KV_LORA
CONV_OFF = ROPE_OFF + QK_ROPE
FOUR_OFF = CONV_OFF + 2 * CONV_W
SGU_OFF = FOUR_OFF + FOURIER_W
GATE_OFF = SGU_OFF + 2 * SGU_W
IN_DIM = GATE_OFF + MIX_W

RMS_EPS = 1e-6
LN_EPS = 1e-5

kernel_name = "hymba_style_mla_conformer_fnet_gmlp_dit"


def _rmsnorm(x, g):
    xf = x.astype(jnp.float32)
    y = xf * lax.rsqrt(jnp.mean(xf * xf, axis=-1, keepdims=True) + RMS_EPS)
    return (y * g.astype(jnp.float32)).astype(x.dtype)


def _layernorm(x, g, b):
    xf = x.astype(jnp.float32)
    mu = jnp.mean(xf, axis=-1, keepdims=True)
    var = jnp.mean(jnp.square(xf - mu), axis=-1, keepdims=True)
    y = (xf - mu) * lax.rsqrt(var + LN_EPS)
    return (y * g.astype(jnp.float32) + b.astype(jnp.float32)).astype(x.dtype)


def _axial_tables(n_tokens):
    rows = n_tokens // GRID_W
    row = jnp.broadcast_to(jnp.arange(rows)[:, None], (rows, GRID_W)).reshape(-1).astype(jnp.float32)
    col = jnp.broadcast_to(jnp.arange(GRID_W)[None, :], (rows, GRID_W)).reshape(-1).astype(jnp.float32)
    inv = ROPE_BASE ** (-jnp.arange(0, AXIS_DIM, 2, dtype=jnp.float32) / AXIS_DIM)
    ang = jnp.stack([row[:, None] * inv, col[:, None] * inv], axis=1)
    return jnp.cos(ang), jnp.sin(ang)


def _rope2d(x, cos, sin):
    shp = (1, cos.shape[0]) + (1,) * (x.ndim - 3) + (2, AXIS_FREQS)
    c = cos.reshape(shp)
    s = sin.reshape(shp)
    xr = x.astype(jnp.float32).reshape(x.shape[:-1] + (2, 2, AXIS_FREQS))
    x1, x2 = xr[..., 0, :], xr[..., 1, :]
    out = jnp.stack([x1 * c - x2 * s, x1 * s + x2 * c], axis=-2)
    return out.reshape(x.shape).astype(x.dtype)


def _mla_q(q_part, p):
    b, n, _ = q_part.shape
    c_q = _rmsnorm(q_part, p["q_norm_g"])
    return (c_q @ p["w_uq"]).reshape(b, n, MLA_HEADS, QK_NOPE + QK_ROPE)


def _mla_kv(kv_part, p):
    b, n, _ = kv_part.shape
    c_kv = _rmsnorm(kv_part[..., :KV_LORA], p["kv_norm_g"])
    kv = (c_kv @ p["w_ukv"]).reshape(b, n, MLA_HEADS, QK_NOPE + V_DIM)
    return kv[..., :QK_NOPE], kv_part[..., KV_LORA:], kv[..., QK_NOPE:]


def _assemble_k(k_nope, k_rope):
    k_r = jnp.broadcast_to(k_rope[:, :, None, :], k_nope.shape[:3] + (QK_ROPE,))
    return jnp.concatenate([k_nope, k_r], axis=-1)


def _attend(q, k, v):
    scale = 1.0 / math.sqrt(QK_NOPE + QK_ROPE)
    s = jnp.einsum("bqhd,bkhd->bhqk", q, k, preferred_element_type=jnp.float32) * scale
    pr = jax.nn.softmax(s, axis=-1)
    return jnp.einsum("bhqk,bkhd->bqhd", pr.astype(v.dtype), v)


def _block_attention(q, k, v):
    b, n, h, dk = q.shape
    nb = n // Q_BLOCK
    qb = q.reshape(b, nb, Q_BLOCK, h, dk).swapaxes(0, 1)
    ob = lax.map(lambda qq: _attend(qq, k, v), qb)
    return ob.swapaxes(0, 1).reshape(b, n, h * V_DIM)


def _conv_module(a, p):
    glu = a[..., :CONV_W] * jax.nn.sigmoid(a[..., CONV_W:])
    y = lax.conv_general_dilated(
        glu, p["conv_w"][:, None, :].astype(glu.dtype), window_strides=(1,),
        padding=[(CONV_K // 2, CONV_K // 2)], dimension_numbers=("NWC", "WIO", "NWC"),
        feature_group_count=CONV_W)
    y = y + p["conv_b"]
    y = jax.nn.silu(_layernorm(y, p["conv_ln_g"], p["conv_ln_b"]))
    return y @ p["w_pw"] + p["b_pw"]


def _fourier(f, p):
    b, n, _ = f.shape
    fg = f.astype(jnp.float32).reshape(b, n, FOURIER_GROUPS, FOURIER_GDIM)
    mixed = jnp.fft.fft2(fg, axes=(1, 3), norm="ortho").real.reshape(b, n, FOURIER_W).astype(f.dtype)
    return mixed @ p["w_fourier"] + p["b_fourier"]


def _spatial_gating(uv, p):
    u, v = uv[..., :SGU_W], uv[..., SGU_W:]
    v = _layernorm(v, p["sgu_ln_g"], p["sgu_ln_b"])
    b, n, _ = v.shape
    vc = v.reshape(b, n // CHUNK, CHUNK, SGU_GROUPS, SGU_GDIM)
    mixed = jnp.einsum("gij,bnjgc->bnigc", p["w_s"], vc) + p["b_s"].T[None, None, :, :, None]
    return u * mixed.reshape(b, n, SGU_W)


def _branches(hp, attn, p):
    conv = _conv_module(hp[..., CONV_OFF:FOUR_OFF], p)
    four = _fourier(hp[..., FOUR_OFF:SGU_OFF], p)
    sgu = _spatial_gating(hp[..., SGU_OFF:GATE_OFF], p)
    y = jnp.concatenate([attn, conv, four, sgu], axis=-1) * jax.nn.silu(hp[..., GATE_OFF:])
    return y @ p["w_out"]


def _layer(x, xc, c, c_ctx, p, cos, sin, need_ctx_out):
    shift, scale, gate = jnp.split(jax.nn.silu(c) @ p["w_ada"] + p["b_ada"], 3, axis=-1)
    shift_c, scale_c, gate_c = jnp.split(jax.nn.silu(c_ctx) @ p["w_ada"] + p["b_ada"], 3, axis=-1)
    h = _rmsnorm(x, p["norm_g"]) * (1.0 + scale[:, None, :]) + shift[:, None, :]
    hc = _rmsnorm(xc, p["norm_g"]) * (1.0 + scale_c) + shift_c

    hp = h @ p["w_in"]
    if need_ctx_out:
        hpc = hc @ p["w_in"]
        ctx_kv = hpc[..., KV_OFF:CONV_OFF]
    else:
        ctx_kv = hc @ p["w_in"][:, KV_OFF:CONV_OFF]

    kn_c, kr_c, v_c = _mla_kv(ctx_kv, p)
    k_c = _assemble_k(kn_c, kr_c)

    q = _mla_q(hp[..., Q_OFF:KV_OFF], p)
    q = jnp.concatenate([q[..., :QK_NOPE], _rope2d(q[..., QK_NOPE:], cos, sin)], axis=-1)
    kn, kr, v = _mla_kv(hp[..., KV_OFF:CONV_OFF], p)
    k = _assemble_k(kn, _rope2d(kr, cos, sin))
    k_all = jnp.concatenate([k, k_c], axis=1)
    v_all = jnp.concatenate([v, v_c], axis=1)
    attn = _block_attention(q, k_all, v_all)

    x = x + gate[:, None, :] * _branches(hp, attn, p)

    if need_ctx_out:
        q_c = _mla_q(hpc[..., Q_OFF:KV_OFF], p)
        b, l = xc.shape[0], xc.shape[1]
        attn_c = _attend(q_c, k_c, v_c).reshape(b, l, ATTN_W)
        xc = xc + gate_c * _branches(hpc, attn_c, p)
    return x, xc


def setup_inputs(seed: int = 0) -> dict:
    key = jax.random.key(seed)
    ks = iter(jax.random.split(key, 40))
    f32 = jnp.float32

    def nrm(shape, s):
        return jax.random.normal(next(ks), shape, f32) * s

    def gain(shape):
        return 1.0 + 0.02 * jax.random.normal(next(ks), shape, f32)

    L = DEPTH
    return {
        "x": nrm((BATCH, SEQ, D_MODEL), 1.0),
        "c": nrm((BATCH, D_MODEL), 1.0),
        "ctx": nrm((BATCH, CTX_LEN, D_MODEL), 1.0),
        "c_ctx": nrm((D_MODEL,), 1.0),
        "w_ada": nrm((L, D_MODEL, 3 * D_MODEL), 0.5 * D_MODEL ** -0.5),
        "b_ada": nrm((L, 3 * D_MODEL), 0.01),
        "norm_g": gain((L, D_MODEL)),
        "w_in": nrm((L, D_MODEL, IN_DIM), D_MODEL ** -0.5),
        "q_norm_g": gain((L, Q_LORA)),
        "w_uq": nrm((L, Q_LORA, MLA_HEADS * (QK_NOPE + QK_ROPE)), Q_LORA ** -0.5),
        "kv_norm_g": gain((L, KV_LORA)),
        "w_ukv": nrm((L, KV_LORA, MLA_HEADS * (QK_NOPE + V_DIM)), KV_LORA ** -0.5),
        "conv_w": nrm((L, CONV_K, CONV_W), CONV_K ** -0.5),
        "conv_b": nrm((L, CONV_W), 0.01),
        "conv_ln_g": gain((L, CONV_W)),
        "conv_ln_b": nrm((L, CONV_W), 0.01),
        "w_pw": nrm((L, CONV_W, CONV_W), CONV_W ** -0.5),
        "b_pw": nrm((L, CONV_W), 0.01),
        "w_fourier": nrm((L, FOURIER_W, FOURIER_W), FOURIER_W ** -0.5),
        "b_fourier": nrm((L, FOURIER_W), 0.01),
        "sgu_ln_g": gain((L, SGU_W)),
        "sgu_ln_b": nrm((L, SGU_W), 0.01),
        "w_s": nrm((L, SGU_GROUPS, CHUNK, CHUNK), 0.5 * CHUNK ** -0.5),
        "b_s": 1.0 + nrm((L, SGU_GROUPS, CHUNK), 0.1),
        "w_out": nrm((L, MIX_W, D_MODEL), MIX_W ** -0.5),
        "final_g": gain((D_MODEL,)),
    }


def reference(x, c, ctx, c_ctx, w_ada, b_ada, norm_g, w_in, q_norm_g, w_uq, kv_norm_g, w_ukv,
              conv_w, conv_b, conv_ln_g, conv_ln_b, w_pw, b_pw, w_fourier, b_fourier,
              sgu_ln_g, sgu_ln_b, w_s, b_s, w_out, final_g):
    n_tokens = x.shape[1]
    cos, sin = _axial_tables(n_tokens)
    xc = ctx
    for i in range(DEPTH):
        p = {
            "w_ada": w_ada[i], "b_ada": b_ada[i], "norm_g": norm_g[i], "w_in": w_in[i],
            "q_norm_g": q_norm_g[i], "w_uq": w_uq[i], "kv_norm_g": kv_norm_g[i], "w_ukv": w_ukv[i],
            "conv_w": conv_w[i], "conv_b": conv_b[i], "conv_ln_g": conv_ln_g[i], "conv_ln_b": conv_ln_b[i],
            "w_pw": w_pw[i], "b_pw": b_pw[i], "w_fourier": w_fourier[i], "b_fourier": b_fourier[i],
            "sgu_ln_g": sgu_ln_g[i], "sgu_ln_b": sgu_ln_b[i], "w_s": w_s[i], "b_s": b_s[i],
            "w_out": w_out[i],
        }
        x, xc = _layer(x, xc, c, c_ctx, p, cos, sin, need_ctx_out=(i < DEPTH - 1))
    return _rmsnorm(x, final_g)
```

```python
import math
from contextlib import ExitStack
import numpy as np
import ml_dtypes
import concourse.bass as bass
import concourse.mybir as mybir
from concourse.bass_utils import run_bass_kernel_spmd

F32 = mybir.dt.float32
BF16 = mybir.dt.bfloat16
DTSIZE = {F32: 4, BF16: 2}
AF = mybir.ActivationFunctionType
ALU = mybir.AluOpType


class Ref:
    __slots__ = ("ap", "tile", "reg")

    def __init__(self, ap, tile, reg):
        self.ap, self.tile, self.reg = ap, tile, reg


class TileW:
    def __init__(self, S, name, shape, dtype, space, base=None, boff=0):
        self.S, self.name, self.shape, self.dtype, self.space = S, name, list(shape), dtype, space
        self.esz = DTSIZE[dtype]
        st = []
        acc = 1
        for d in reversed(self.shape[1:]):
            st.append(acc)
            acc *= d
        self.strides = list(reversed(st))
        self.free_elems = acc
        self.boff = boff
        if base is None:
            self.hist = []
            if space == "sbuf":
                self.t = S.stack.enter_context(S.nc.sbuf_tensor(name, self.shape, dtype))
            else:
                self.t = S.stack.enter_context(S.nc.psum_tensor(name, self.shape, dtype))
            self.apbase = None
        else:
            self.hist = base.hist
            assert boff % base.esz == 0 and boff % self.esz == 0
            nb = acc * self.esz
            assert boff + nb <= base.free_elems * base.esz, (name, boff, nb)
            a = base.t[:, boff // base.esz:(boff + nb) // base.esz]
            if dtype != base.dtype:
                a = a.bitcast(dtype)
            if len(self.shape) == 3:
                a = a.rearrange("p (a b) -> p a b", a=self.shape[1])
            elif len(self.shape) == 4:
                a = a.rearrange("p (a b c) -> p a b c", a=self.shape[1], b=self.shape[2])
            self.apbase = a

    def __getitem__(self, idx):
        if not isinstance(idx, tuple):
            idx = (idx,)
        idx = list(idx) + [slice(None)] * (len(self.shape) - len(idx))
        p = idx[0]
        if isinstance(p, int):
            p0, p1 = p, p + 1
        else:
            p0, p1, _ = p.indices(self.shape[0])
        lo = 0
        hi = 0
        for d, (ix, stv) in enumerate(zip(idx[1:], self.strides)):
            n = self.shape[d + 1]
            if isinstance(ix, int):
                lo += ix * stv
                hi += ix * stv
            else:
                a, b, s = ix.indices(n)
                cnt = max(0, (b - a + s - 1) // s)
                lo += a * stv
                hi += (a + (cnt - 1) * s) * stv
        if self.space == "psum":
            b0 = (lo * self.esz) // 2048 * 2048
            b1 = ((hi + 1) * self.esz + 2047) // 2048 * 2048
            reg = (0, 128, b0, b1)
        else:
            reg = (p0, p1, self.boff + lo * self.esz, self.boff + (hi + 1) * self.esz)
        src = self.t if self.apbase is None else self.apbase
        return Ref(src[tuple(idx)], self, reg)


def _overlap(a, b):
    return a[0] < b[1] and b[0] < a[1] and a[2] < b[3] and b[2] < a[3]


def _contains(a, b):
    return a[0] <= b[0] and a[1] >= b[1] and a[2] <= b[2] and a[3] >= b[3]


class Op:
    __slots__ = ("eng", "fn", "waits", "inc", "dma", "snap")

    def __init__(self, eng, fn):
        self.eng, self.fn, self.waits, self.inc, self.dma, self.snap = eng, fn, [], False, None, None


NDMASEM = 24
SKIP = set()


class Sched:
    def __init__(self, nc, stack):
        self.nc, self.stack = nc, stack
        self.engs = {"pe": nc.tensor, "act": nc.scalar, "dve": nc.vector, "pool": nc.gpsimd, "sp": nc.sync}
        self.ops = {e: [] for e in self.engs}
        self.vc = {e: {} for e in self.engs}
        self.dma_seq = [0] * NDMASEM
        self.dma_snap = [dict() for _ in range(NDMASEM)]
        self.dma_rr = 0

    def sb(self, name, shape, dtype):
        return TileW(self, name, shape, dtype, "sbuf")

    def ps(self, name, shape, dtype=F32):
        return TileW(self, name, shape, dtype, "psum")

    def view(self, base, boff, shape, dtype, name="v"):
        return TileW(self, name, shape, dtype, "sbuf", base=base, boff=boff)

    def _deps(self, reads, writes, eng=None):
        deps = set()
        for r in reads:
            ps = r.tile.space == "psum"
            for (reg, kind, e, i) in r.tile.hist:
                if (kind == "w" or (ps and e != eng)) and _overlap(reg, r.reg):
                    deps.add((e, i))
        for w in writes:
            for (reg, kind, e, i) in w.tile.hist:
                if _overlap(reg, w.reg):
                    deps.add((e, i))
        return deps

    def _snap_of(self, e, i):
        if e[0] == "q":
            return self.dma_snap[int(e[1:])][i]
        return self.ops[e][i].snap

    def _apply_waits(self, eng, deps):
        vc = self.vc[eng]
        waits = []
        for (e, i) in sorted(deps, key=lambda x: -x[1]):
            if e == eng and eng == "pe":
                continue
            if vc.get(e, -1) >= i:
                continue
            waits.append((e, i))
            for k, v in self._snap_of(e, i).items():
                if vc.get(k, -1) < v:
                    vc[k] = v
            if e[0] != "q":
                self.ops[e][i].inc = True
        return waits

    def _record(self, reads, writes, e, i):
        for r in reads:
            r.tile.hist.append((r.reg, "r", e, i))
        for w in writes:
            h = w.tile.hist
            h[:] = [x for x in h if not _contains(w.reg, x[0])]
            h.append((w.reg, "w", e, i))

    def op(self, eng, fn, reads=(), writes=()):
        reads = [r for r in reads if isinstance(r, Ref)]
        writes = [w for w in writes if isinstance(w, Ref)]
        deps = self._deps(reads, writes, eng)
        o = Op(eng, fn)
        o.waits = self._apply_waits(eng, deps)
        idx = len(self.ops[eng])
        o.snap = dict(self.vc[eng])
        o.snap[eng] = idx
        self.ops[eng].append(o)
        self._record(reads, writes, eng, idx)
        return o

    def dma(self, queue, out, in_):
        k = self.dma_rr
        self.dma_rr = (self.dma_rr + 1) % NDMASEM
        pe = "q%d" % k
        reads = [in_] if isinstance(in_, Ref) else []
        writes = [out] if isinstance(out, Ref) else []
        deps = self._deps(reads, writes)
        seq = self.dma_seq[k]
        if seq > 0:
            deps.add((pe, seq - 1))
        oap = out.ap if isinstance(out, Ref) else out
        iap = in_.ap if isinstance(in_, Ref) else in_

        def fn(E, oap=oap, iap=iap):
            return E.dma_start(out=oap, in_=iap)
        o = Op(queue, fn)
        o.waits = self._apply_waits(queue, deps)
        o.dma = (k, seq)
        idx = len(self.ops[queue])
        o.snap = dict(self.vc[queue])
        self.ops[queue].append(o)
        snap = dict(self.vc[queue])
        snap[pe] = seq
        self.dma_snap[k][seq] = snap
        self.dma_seq[k] = seq + 1
        self._record(reads, writes, pe, seq)
        return o

    def finish(self):
        deps = set()
        for k in range(NDMASEM):
            if self.dma_seq[k] > 0:
                deps.add(("q%d" % k, self.dma_seq[k] - 1))
        o = Op("sp", None)
        o.waits = self._apply_waits("sp", deps)
        self.ops["sp"].append(o)

    def emit(self):
        nc = self.nc
        sems = {e: self.stack.enter_context(nc.semaphore("s_" + e)) for e in self.engs}
        dsems = [self.stack.enter_context(nc.semaphore("d_%d" % k)) for k in range(NDMASEM)]
        rank = {}
        for e, ops in self.ops.items():
            c = 0
            r = []
            for o in ops:
                if o.inc:
                    c += 1
                r.append(c)
            rank[e] = r
        nwait = ninst = 0
        for e in ["sp", "pool", "pe", "act", "dve"]:
            E = self.engs[e]
            for o in self.ops[e]:
                for (e2, i2) in o.waits:
                    if e2[0] == "q":
                        E.wait_ge(dsems[int(e2[1:])], 16 * (i2 + 1))
                    else:
                        E.wait_ge(sems[e2], rank[e2][i2])
                    nwait += 1
                if o.fn is None:
                    continue
                inst = o.fn(E)
                ninst += 1
                if o.dma is not None:
                    inst.then_inc(dsems[o.dma[0]], 16)
                elif o.inc:
                    inst.then_inc(sems[e], 1)
        self.stats = dict(ninst=ninst, nwait=nwait, per_eng={e: len(v) for e, v in self.ops.items()})
        return self.stats


D = 1024
NLAT = 2048
NCTX = 256
T = NLAT + NCTX
DEPTH = 4
IN_DIM = 2720
TCH = [(0, 512), (512, 512), (1024, 512), (1536, 512), (2048, 256)]
SM_SCALE = 1.0 / math.sqrt(96.0)
RMS_EPS = 1e-6
LN_EPS = 1e-5
PV_NG, PV_BADA, PV_GQ, PV_GKV, PV_CB, PV_LG, PV_LB, PV_BPW, PV_BFO, PV_CW = 0, 16, 64, 66, 67, 69, 71, 73, 75, 77
NPL = 77 + 62
PV_FINAL = DEPTH * NPL
PV_C = PV_FINAL + 8
NPV = PV_C + 16
C_Q, C_KVR, C_CA, C_CG, C_FO, C_U, C_V, C_GATE = 0, 256, 416, 672, 928, 1184, 1440, 1696


def build(nlayers=DEPTH, dbg=False, phases="ABCDEFGH"):
    nc = bass.Bass("TRN2", target_bir_lowering=False)

    def din(name, shape, dt=F32):
        return nc.dram_tensor(name, list(shape), dt, kind="ExternalInput").ap()

    xT_d = din("xT", [D, T])
    pv_d = din("pv", [128, NPV])
    w_ada_d = din("w_ada", [DEPTH, D, 3 * D])
    w_in_d = din("w_in", [DEPTH, D, IN_DIM])
    w_ropeB_d = din("w_ropeB", [DEPTH, D, 96])
    w_uq_d = din("w_uq", [DEPTH, 256, 384])
    w_uqs_d = din("w_uqs", [DEPTH, 256, 384])
    w_ukv_d = din("w_ukv2", [DEPTH, 128, 512])
    w_pw_d = din("w_pw", [DEPTH, 256, 256])
    w_fo_d = din("w_fourier", [DEPTH, 256, 256])
    w_sT_d = din("w_sT", [DEPTH, 128, 512])
    b_s_d = din("b_s", [DEPTH, 1, 512])
    sgb_d = din("sgu_gb", [DEPTH, 128, 512])
    w_out_d = din("w_out", [DEPTH, D, D])
    rope_d = din("ropeT", [2, 128, NLAT])
    dftL_d = din("dftL", [2, NLAT, NLAT], BF16)
    dftC_d = din("dftC", [2, NCTX, NCTX], BF16)
    cs128_d = din("cs128", [128, 256], BF16)
    ident_d = din("ident", [128, 128], BF16)
    outT_d = nc.dram_tensor("outT", [D, NLAT], F32, kind="ExternalOutput").ap()
    dbg_d = None
    if dbg:
        dbg_d = nc.dram_tensor("dbgy", [128, 8 * T], F32, kind="ExternalOutput").ap()
        dbg2_d = nc.dram_tensor("dbg2", [128, 128], F32, kind="ExternalOutput").ap()

    with ExitStack() as st:
        S = Sched(nc, st)
        xT = S.sb("xTs", [128, 8, T], F32)
        hT = S.sb("hTs", [128, 8, T], BF16)
        ARENA_B = 81 * 1024
        arena = S.sb("arena", [128, ARENA_B // 2], BF16)
        wg = [S.sb("wg%d" % i, [128, 8, 256], BF16) for i in range(3)]
        pv = S.sb("pvs", [128, NPV], F32)
        smalls = [S.sb("small%d" % i, [128, 256], F32) for i in range(2)]
        th16t = S.sb("th16t", [128, 16], F32)
        cur = {"small": smalls[0]}
        scb = S.sb("scb", [128, 16], BF16)
        onesb = S.sb("onesb", [128, 4, 128], BF16)
        onesf = S.sb("onesf", [128, 64], F32)
        cm05 = S.sb("cm05", [128, 8], F32)
        ident = S.sb("identb", [128, 128], BF16)
        cs128 = S.sb("cs128s", [128, 256], BF16)
        PP = [S.ps("pp%d" % i, [128, 1024]) for i in range(4)]

        def bank(i):
            return PP[i // 2], (i % 2) * 512

        def bk(i, p0=0, p1=128, c0=0, c1=512):
            t, o = bank(i)
            return t[p0:p1, o + c0:o + c1]

        y_off = lambda g: g * T * 2
        yT = S.view(arena, 0, [128, 8, T], BF16, "yT")
        E0 = 8 * T * 2

        def mm(out, lhsT, rhs, start, stop):
            S.op("pe", lambda E: E.matmul(out.ap, lhsT.ap, rhs.ap, start=start, stop=stop), [lhsT, rhs], [out])

        def act(out, in_, func, scale=1.0, bias=0.0, eng="act"):
            sa = scale.ap if isinstance(scale, Ref) else scale
            ba = bias.ap if isinstance(bias, Ref) else bias
            S.op("act", lambda E: E.activation(out.ap, in_.ap, func, bias=ba, scale=sa), [in_, scale, bias], [out])

        def tt(eng, out, in0, in1, op):
            S.op(eng, lambda E: E.tensor_tensor(out.ap, in0.ap, in1.ap, op), [in0, in1], [out])

        def ts(eng, out, in0, s1, s2, op0, op1=None):
            a1 = s1.ap if isinstance(s1, Ref) else s1
            a2 = s2.ap if isinstance(s2, Ref) else s2
            if op1 is None:
                S.op(eng, lambda E: E.tensor_scalar(out.ap, in0.ap, a1, None, op0), [in0, s1], [out])
            else:
                S.op(eng, lambda E: E.tensor_scalar(out.ap, in0.ap, a1, a2, op0, op1), [in0, s1, s2], [out])

        def stt(eng, out, in0, sc, in1, op0, op1):
            a = sc.ap if isinstance(sc, Ref) else sc
            S.op(eng, lambda E: E.scalar_tensor_tensor(out.ap, in0.ap, a, in1.ap, op0, op1), [in0, sc, in1], [out])

        def cp(eng, out, in_):
            if eng == "act":
                act(out, in_, AF.Copy)
            else:
                S.op(eng, lambda E: E.tensor_copy(out.ap, in_.ap), [in_], [out])

        def memset(eng, out, val):
            S.op(eng, lambda E: E.memset(out.ap, val), [], [out])

        wg_rr = [0]
        marks = []

        def mark(name):
            marks.append((name, len(S.ops['pe']), len(S.ops['act']), len(S.ops['dve'])))

        def wload(src2d, width=256):
            b = wg[wg_rr[0] % 3]
            wg_rr[0] += 1
            S.dma("pool", b[:, :, 0:width], src2d.rearrange("(k p) w -> p k w", p=128))
            return b

        S.dma("sp", pv[:, :], pv_d)
        S.dma("sp", ident[:, :], ident_d)
        S.dma("sp", cs128[:, :], cs128_d)
        for k in range(8):
            S.dma("sp", xT[:, k, :], xT_d[k * 128:(k + 1) * 128, :])
        memset("pool", onesb[:, 0, :], 1.0 / 1024)
        memset("pool", onesb[:, 1, :], 1.0 / 256)
        memset("pool", onesb[:, 2, :], 1.0 / 128)
        memset("pool", onesb[:, 3, :], 1.0)
        memset("pool", onesf[:, :], 1.0)
        memset("pool", cm05[:, 0:1], RMS_EPS)
        memset("pool", cm05[:, 1:2], LN_EPS)
        cpk = pv[:, PV_C:PV_C + 16]
        th16 = th16t[:, :]
        act(th16, cpk, AF.Tanh, scale=0.5)
        stt("dve", scb[:, :], th16, 1.0, cpk, ALU.add, ALU.mult)

        def shift_c(k, s):
            return cur["small"][:, k * 2 + s:k * 2 + s + 1]

        def gs_c(k, s):
            return cur["small"][:, 48 + k * 2 + s:48 + k * 2 + s + 1]

        def gh_c(k, s):
            return cur["small"][:, 64 + k * 2 + s:64 + k * 2 + s + 1]

        def phaseA(l_, bufs=None):
            sm = smalls[l_ % 2]
            pvl_ = lambda c0, c1: pv[:, l_ * NPL + c0:l_ * NPL + c1]
            for jp in range(12):
                if bufs is None:
                    wb = wload(w_ada_d[l_, :, jp * 256:(jp + 1) * 256])
                else:
                    wb = bufs[jp % len(bufs)]
                    S.dma("pool", wb[:, :, :], w_ada_d[l_, :, jp * 256:(jp + 1) * 256].rearrange("(k p) w -> p k w", p=128))
                for jj in range(2):
                    for k in range(8):
                        mm(bk(7, 0, 128, jj * 2, jj * 2 + 2), wb[:, k, jj * 128:(jj + 1) * 128], scb[:, k * 2:k * 2 + 2], k == 0, k == 7)
                cp("dve", sm[:, 150 + jp * 4:150 + jp * 4 + 4], bk(7, 0, 128, 0, 4))
                yield
            stt("dve", sm[:, 0:48], sm[:, 150:198], 0.5, pvl_(PV_BADA, PV_BADA + 48), ALU.mult, ALU.add)
            stt("dve", sm[:, 48:64], sm[:, 16:32], 1.0, pvl_(PV_NG, PV_NG + 16), ALU.add, ALU.mult)
            ts("dve", sm[:, 64:80], sm[:, 32:48], 0.5, None, ALU.mult)
            ts("dve", sm[:, 80:82], pvl_(PV_LG, PV_LG + 2), 0.5, None, ALU.mult)
            ts("dve", sm[:, 82:84], pvl_(PV_LB, PV_LB + 2), 0.5, None, ALU.mult)
            ts("dve", sm[:, 84:146], pvl_(PV_CW, PV_CW + 62), 0.5, None, ALU.mult)
            yield

        nextA = {"gen": None}

        def stepA(nsteps=1):
            g_ = nextA["gen"]
            if g_ is None:
                return
            for _ in range(nsteps):
                try:
                    next(g_)
                except StopIteration:
                    nextA["gen"] = None
                    return

        class Alloc:
            def __init__(self, start):
                self.off = start

            def get(self, shape, dtype, name="t"):
                n = DTSIZE[dtype]
                for d in shape[1:]:
                    n *= d
                n = (n + 63) // 64 * 64
                v = S.view(arena, self.off, shape, dtype, name)
                self.off += n
                assert self.off <= ARENA_B, (name, self.off)
                return v

        def recip(out, in_):
            S.op("dve", lambda E: E.reciprocal(out.ap, in_.ap), [in_], [out])

        def epsc(eps):
            return cm05[:, 0:1] if eps == RMS_EPS else cm05[:, 1:2]

        def rstd_from(pb_ref, n, eps, ms, rstd):
            act(ms[:, :n], pb_ref, AF.Sqrt, bias=epsc(eps))
            recip(rstd[:, :n], ms[:, :n])

        def gate_s2(wgate, gl, t0, n, bi, th, s2out):
            for k in range(8):
                mm(bk(bi, 0, 128, 0, n), wgate[:, k, gl * 128:(gl + 1) * 128], hT[:, k, t0:t0 + n], k == 0, k == 7)
            act(th[:, :n], bk(bi, 0, 128, 0, n), AF.Tanh, scale=0.5)
            stt("dve", s2out, th[:, :n], 1.0, bk(bi, 0, 128, 0, n), ALU.add, ALU.mult)

        for l in range(nlayers):
            pvl = lambda c0, c1: pv[:, l * NPL + c0:l * NPL + c1]
            mark('L%dA' % l)
            if "A" in phases:
                if l == 0:
                    for _ in phaseA(0):
                        pass
                else:
                    stepA(100)
            cur["small"] = smalls[l % 2]
            small = cur["small"]

            mark('L%dB' % l)
            if "B" in phases:
                A = Alloc(E0)
                sqb = [A.get([128, 512], BF16, "sqb%d" % i) for i in range(3)]
                msb = [A.get([128, 512], F32, "ms%d" % i) for i in range(2)]
                rsb = [A.get([128, 512], F32, "rs%d" % i) for i in range(2)]
                tmpb = [A.get([128, 512], F32, "tmp%d" % i) for i in range(3)]
                def sq_mm(tc_):
                    t0_, n_ = TCH[tc_]
                    for k in range(8):
                        sq = sqb[k % 3]
                        act(sq[:, :n_], xT[:, k, t0_:t0_ + n_], AF.Square)
                        mm(bk(tc_ % 4, 0, 128, 0, n_), onesb[:, 0, :], sq[:, :n_], k == 0, k == 7)

                sq_mm(0)
                for tc, (t0, n) in enumerate(TCH):
                    s = 0 if tc < 4 else 1
                    bi = tc % 4
                    ms, rs = msb[tc % 2], rsb[tc % 2]
                    rstd_from(bk(bi, 0, 128, 0, n), n, RMS_EPS, ms, rs)
                    if tc + 1 < len(TCH):
                        sq_mm(tc + 1)
                    for k in range(8):
                        tmp = tmpb[k % 3]
                        stt("dve", tmp[:, :n], xT[:, k, t0:t0 + n], gs_c(k, s), rs[:, :n], ALU.mult, ALU.mult)
                        act(hT[:, k, t0:t0 + n], tmp[:, :n], AF.Identity, bias=shift_c(k, s))

            mark('L%dC' % l)
            if "C" in phases:
                A = Alloc(2 * T * 2)
                kT = A.get([128, 4, T], BF16, "kT")
                Vt4 = A.get([128, 18, 4, 128], BF16, "Vt4")
                tab = A.get([128, 2, 512], F32, "tab")
                f32t = [A.get([128, 512], F32, "f%d" % i) for i in range(4)]
                dbase = A.off
                NPT = 3
                PT = [A.get([128, 1024], BF16, "PT%d" % i) for i in range(NPT)]
                cqT = A.get([128, 2, 512], BF16, "cqT")
                s2g = A.get([128, 2, 512], F32, "s2g")
                qT = A.get([128, 2, 4, 512], BF16, "qT")
                wsm = A.get([128, 2, 384], BF16, "wuq")
                wsms = A.get([128, 2, 384], BF16, "wuqs")
                A2 = Alloc(dbase)
                ckvT = A2.get([128, T], BF16, "ckvT")
                wukv = A2.get([128, 512], BF16, "wukv")
                wrB = A2.get([128, 8, 96], BF16, "wrB")
                assert A2.off <= dbase + NPT * 2048 + 2048
                A3 = Alloc(dbase + NPT * 2048 + 2048)
                sqbC = [A3.get([128, 512], BF16, "sqbC%d" % i) for i in range(2)]
                t2C = A3.get([128, 512], F32, "t2C")
                sqb = PT
                memset("dve", kT[:, :, :], 0.0)
                memset("dve", qT[:, :, :, :], 0.0)
                for j in range(18):
                    memset("dve", Vt4[:, j, :, 64:128], 1.0)
                S.dma("pool", wsm[:, :, :], w_uq_d[l].rearrange("(k p) w -> p k w", p=128))
                S.dma("pool", wsms[:, :, :], w_uqs_d[l].rearrange("(k p) w -> p k w", p=128))
                S.dma("pool", wukv[:, :], w_ukv_d[l])
                S.dma("pool", wrB[:, :, :], w_ropeB_d[l].rearrange("(k p) w -> p k w", p=128))
                wkvr = wload(w_in_d[l, :, C_KVR:C_KVR + 256])
                gkv = pvl(PV_GKV, PV_GKV + 1)
                b_kv, b_a, b_b, b_n = 0, 1, 2, 3

                def kv_mm(tc_):
                    t0_, n_ = TCH[tc_]
                    for k in range(8):
                        mm(bk(b_kv, 0, 128, 0, n_), wkvr[:, k, 0:128], hT[:, k, t0_:t0_ + n_], k == 0, k == 7)
                    for k in range(8):
                        mm(bk(b_a, 0, 96, 0, n_), wkvr[:, k, 64:160], hT[:, k, t0_:t0_ + n_], k == 0, k == 7)
                    if tc_ < 4:
                        for k in range(8):
                            mm(bk(b_b, 0, 96, 0, n_), wrB[:, k, :], hT[:, k, t0_:t0_ + n_], k == 0, k == 7)

                kv_mm(0)
                for tc, (t0, n) in enumerate(TCH):
                    lat = tc < 4
                    tb = tab
                    if lat:
                        S.dma("sp", tb[64:96, 0, :], rope_d[0, 64:96, t0:t0 + 512])
                        S.dma("sp", tb[64:96, 1, :], rope_d[1, 64:96, t0:t0 + 512])
                    sq = sqbC[tc % 2]
                    act(sq[:, :n], bk(b_kv, 0, 128, 0, n), AF.Square)
                    kvg = f32t[0]
                    act(kvg[:, :n], bk(b_kv, 0, 128, 0, n), AF.Copy, scale=gkv)
                    if lat:
                        t1, t2 = f32t[3], t2C
                        tt("dve", t1[64:96, :n], bk(b_a, 64, 96, 0, n), tb[64:96, 0, :n], ALU.mult)
                        tt("dve", t2[64:96, :n], bk(b_b, 64, 96, 0, n), tb[64:96, 1, :n], ALU.mult)
                        tt("pool", kT[64:96, 0, t0:t0 + n], t1[64:96, :n], t2[64:96, :n], ALU.add)
                        for h in range(1, 4):
                            cp("pool", kT[64:96, h, t0:t0 + n], kT[64:96, 0, t0:t0 + n])
                    else:
                        for h in range(4):
                            cp("act" if h % 2 else "dve", kT[64:96, h, t0:t0 + n], bk(b_a, 64, 96, 0, n))
                    mm(bk(b_n, 0, 128, 0, n), onesb[:, 2, :], sq[:, :n], True, True)
                    if tc + 1 < len(TCH):
                        kv_mm(tc + 1)
                    rstd_from(bk(b_n, 0, 128, 0, n), n, RMS_EPS, f32t[1], f32t[2])
                    tt("dve", ckvT[:, t0:t0 + n], kvg[:, :n], f32t[2][:, :n], ALU.mult)
                    for h in range(0 if "knope" not in SKIP else 4, 4):
                        bi = 4 + (h % 2)
                        mm(bk(bi, 0, 64, 0, n), wukv[:, h * 64:(h + 1) * 64], ckvT[:, t0:t0 + n], True, True)
                        cp("act", kT[0:64, h, t0:t0 + n], bk(bi, 0, 64, 0, n))
                    for jj in range(n // 128 if "v" not in SKIP else 0):
                        j = t0 // 128 + jj
                        bi = 6 + (jj // 2) % 2
                        pbv = bk(bi, 0, 128, (jj % 2) * 256, (jj % 2) * 256 + 256)
                        mm(pbv, ckvT[:, j * 128:(j + 1) * 128], wukv[:, 256:512], True, True)
                        pbv3 = Ref(pbv.ap.rearrange("p (b c) -> p b c", b=4), pbv.tile, pbv.reg)
                        cp("act", Vt4[:, j, :, 0:64], pbv3)
                mark('L%dD' % l)
                if "D" in phases:
                    wq = wload(w_in_d[l, :, C_Q:C_Q + 256])
                    wgt = wload(w_in_d[l, :, C_GATE:C_GATE + 256])
                    gq = lambda g: pvl(PV_GQ + g, PV_GQ + g + 1)
                    pt_rr = 0
                    for qc, (t0, n) in enumerate(TCH):
                        lat = qc < 4
                        qb = qc % 2
                        if lat:
                            tb = tab
                            S.dma("sp", tb[64:96, 0, :], rope_d[0, 64:96, t0:t0 + 512])
                            S.dma("sp", tb[64:96, 1, :], rope_d[1, 64:96, t0:t0 + 512])
                        for g in range(2):
                            for k in range(8):
                                mm(bk(g, 0, 128, 0, n), wq[:, k, g * 128:(g + 1) * 128], hT[:, k, t0:t0 + n], k == 0, k == 7)
                        for g in range(2):
                            act(sqb[g][:, :n], bk(g, 0, 128, 0, n), AF.Square)
                            ts("dve", f32t[g][:, :n], bk(g, 0, 128, 0, n), gq(g), None, ALU.mult)
                            mm(bk(2, 0, 128, 0, n), onesb[:, 1, :], sqb[g][:, :n], g == 0, g == 1)
                        for g in range(2):
                            pbg = bk(4 + g, 0, 128, 0, n)
                            for k in range(8):
                                mm(pbg, wgt[:, k, g * 128:(g + 1) * 128], hT[:, k, t0:t0 + n], k == 0, k == 7)
                        rstd_from(bk(2, 0, 128, 0, n), n, RMS_EPS, f32t[2], f32t[3])
                        for g in range(2):
                            tt("dve", cqT[:, g, :n], f32t[g][:, :n], f32t[3][:, :n], ALU.mult)
                        for g in range(2):
                            pbg = bk(4 + g, 0, 128, 0, n)
                            act(s2g[:, g, :n], pbg, AF.Tanh, scale=0.5)
                            stt("dve", s2g[:, g, :n], s2g[:, g, :n], 1.0, pbg, ALU.add, ALU.mult)
                        for h in range(4):
                            ba, bb = 0 + (h % 2) * 2, 1 + (h % 2) * 2
                            for g in range(2):
                                mm(bk(ba, 0, 96, 0, n), wsm[:, g, h * 96:(h + 1) * 96], cqT[:, g, :n], g == 0, g == 1)
                            if lat:
                                for g in range(2):
                                    mm(bk(bb, 0, 96, 0, n), wsms[:, g, h * 96:(h + 1) * 96], cqT[:, g, :n], g == 0, g == 1)
                            cp("act", qT[0:64, qb, h, :n], bk(ba, 0, 64, 0, n))
                            if lat:
                                t1, t2 = f32t[0], f32t[1]
                                tt("dve", t1[64:96, :n], bk(ba, 64, 96, 0, n), tb[64:96, 0, :n], ALU.mult)
                                tt("dve", t2[64:96, :n], bk(bb, 64, 96, 0, n), tb[64:96, 1, :n], ALU.mult)
                                tt("pool", qT[64:96, qb, h, :n], t1[64:96, :n], t2[64:96, :n], ALU.add)
                            else:
                                cp("dve", qT[64:96, qb, h, :n], bk(ba, 64, 96, 0, n))
                        kts = list(range(18)) if lat else [16, 17]
                        npair = len(kts) // 2
                        items = [(h, pi) for h in range(4) for pi in range(npair)]
                        LA = 2

                        def issue_S(ii):
                            h_, pi_ = items[ii]
                            sp_ = PP[ii % 3]
                            k0_, k1_ = kts[2 * pi_], kts[2 * pi_ + 1]
                            mm(sp_[:, 0:n], kT[:, h_, k0_ * 128:(k0_ + 1) * 128], qT[:, qb, h_, :n], True, True)
                            mm(sp_[:, 512:512 + n], kT[:, h_, k1_ * 128:(k1_ + 1) * 128], qT[:, qb, h_, :n], True, True)

                        for ii in range(min(LA, len(items))):
                            issue_S(ii)
                        for ii, (h, pi) in enumerate(items):
                            if ii + LA < len(items):
                                issue_S(ii + LA)
                            sp_t = PP[ii % 3]
                            P = PT[ii % 3]
                            ob = 6 + (h % 2)
                            if n == 512:
                                act(P[:, 0:1024], sp_t[:, 0:1024], AF.Exp, scale=SM_SCALE)
                            else:
                                act(P[:, 0:n], sp_t[:, 0:n], AF.Exp, scale=SM_SCALE)
                                act(P[:, 512:512 + n], sp_t[:, 512:512 + n], AF.Exp, scale=SM_SCALE)
                            for jj, kt in enumerate((kts[2 * pi], kts[2 * pi + 1])):
                                first = (pi == 0 and jj == 0)
                                last = (pi == npair - 1 and jj == 1)
                                mm(bk(ob, 0, 128, 0, n), Vt4[:, kt, h, :], P[:, jj * 512:jj * 512 + n], first, last)
                            if pi == npair - 1:
                                rden = f32t[2]
                                recip(rden[64:128, :n], bk(ob, 64, 128, 0, n))
                                r0 = (h % 2) * 64
                                on = f32t[h % 2]
                                tt("dve", on[r0:r0 + 64, :n], bk(ob, 0, 64, 0, n), rden[64:128, :n], ALU.mult)
                                tt("pool", yT[r0:r0 + 64, h // 2, t0:t0 + n], on[r0:r0 + 64, :n], s2g[r0:r0 + 64, h // 2, :n], ALU.mult)

            mark('L%dE' % l)
            if "E" in phases:
                A = Alloc(4 * T * 2)
                glu = A.get([128, 2, 2364], BF16, "glu")
                dg = A.get([128, 2, 31, 128], BF16, "dg")
                wpw = A.get([128, 2, 256], BF16, "wpw")
                cvfs = [A.get([128, 2, 512], F32, "cvf%d" % i) for i in range(2)]
                cvbs = [A.get([128, 2, 512], BF16, "cvb%d" % i) for i in range(2)]
                sq2s = [A.get([128, 2, 512], BF16, "sq2%d" % i) for i in range(2)]
                gts = [A.get([128, 512], F32, "gts%d" % i) for i in range(2)]
                s2b = A.get([128, 2, 512], BF16, "s2b")
                f32t = [A.get([128, 512], F32, "f%d" % i) for i in range(7)]
                S.dma("pool", wpw[:, :, :], w_pw_d[l].rearrange("(k p) w -> p k w", p=128))
                GOFF = [15, 15 + 2048 + 30]
                for c in range(2):
                    memset("pool", glu[:, c, 0:15], 0.0)
                    memset("pool", glu[:, c, 15 + 2048:15 + 2048 + 30], 0.0)
                    memset("pool", glu[:, c, 2364 - 15:2364], 0.0)
                dgi = [0]
                wca = wload(w_in_d[l, :, C_CA:C_CA + 256])
                wcg = wload(w_in_d[l, :, C_CG:C_CG + 256])
                for tc, (t0, n) in enumerate(TCH):
                    po = GOFF[0] + t0 if tc < 4 else GOFF[1]
                    for c in range(2):
                        ba, bg = (c * 2) % 4, (c * 2 + 1) % 4
                        for k in range(8):
                            mm(bk(ba, 0, 128, 0, n), wca[:, k, c * 128:(c + 1) * 128], hT[:, k, t0:t0 + n], k == 0, k == 7)
                        for k in range(8):
                            mm(bk(bg, 0, 128, 0, n), wcg[:, k, c * 128:(c + 1) * 128], hT[:, k, t0:t0 + n], k == 0, k == 7)
                        th = f32t[c]
                        act(th[:, :n], bk(bg, 0, 128, 0, n), AF.Tanh, scale=0.5)
                        stt("dve", glu[:, c, po:po + n], th[:, :n], 1.0, bk(ba, 0, 128, 0, n), ALU.add, ALU.mult)
                        for _ in range(7):
                            if dgi[0] < 62:
                                c_, k_ = divmod(dgi[0], 31)
                                ts("dve", dg[:, c_, k_, :], ident[:, :], small[:, 84 + c_ * 31 + k_:84 + c_ * 31 + k_ + 1], None, ALU.mult)
                                dgi[0] += 1
                wgt = wload(w_in_d[l, :, C_GATE + 256:C_GATE + 512])
                def conv_mm(tc_):
                    t0_, n_ = TCH[tc_]
                    po_ = t0_ if tc_ < 4 else GOFF[1] - 15
                    for c in range(2):
                        for k in range(31):
                            mm(bk(4 + c, 0, 128, 0, n_), dg[:, c, k, :], glu[:, c, po_ + k:po_ + k + n_], k == 0, k == 30)

                conv_mm(0)
                for tc, (t0, n) in enumerate(TCH):
                    db_ = tc % 2
                    cvf, cvb, sq2 = cvfs[db_], cvbs[db_], sq2s[db_]
                    for c in range(2):
                        bi = 4 + c
                        cb = pvl(PV_CB + c, PV_CB + c + 1)
                        act(cvf[:, c, :n], bk(bi, 0, 128, 0, n), AF.Identity, bias=cb)
                        act(sq2[:, c, :n], bk(bi, 0, 128, 0, n), AF.Square, bias=cb)
                        cp("dve", cvb[:, c, :n], cvf[:, c, :n])
                    for c in range(2):
                        mm(bk(6, 0, 128, 0, n), onesb[:, 1, :], cvb[:, c, :n], c == 0, c == 1)
                    for c in range(2):
                        mm(bk(7, 0, 128, 0, n), onesb[:, 1, :], sq2[:, c, :n], c == 0, c == 1)
                    for oc in range(2):
                        s2 = gts[oc]
                        for k in range(8):
                            mm(bk(2 + oc, 0, 128, 0, n), wgt[:, k, oc * 128:(oc + 1) * 128], hT[:, k, t0:t0 + n], k == 0, k == 7)
                        act(s2[:, :n], bk(2 + oc, 0, 128, 0, n), AF.Tanh, scale=0.5)
                        stt("dve", s2[:, :n], s2[:, :n], 1.0, bk(2 + oc, 0, 128, 0, n), ALU.add, ALU.mult)
                    if tc + 1 < len(TCH):
                        conv_mm(tc + 1)
                    mean, vv = f32t[0], f32t[1]
                    zs, ths, ub = [f32t[2], f32t[3]], [f32t[4], f32t[5]], f32t[6]
                    halves = [(0, n // 2), (n // 2, n)]
                    for (c0, c1) in halves:
                        cp("act", mean[:, c0:c1], bk(6, 0, 128, c0, c1))
                    for (c0, c1) in halves:
                        tt("dve", vv[:, c0:c1], mean[:, c0:c1], mean[:, c0:c1], ALU.mult)
                        stt("dve", vv[:, c0:c1], bk(7, 0, 128, c0, c1), LN_EPS, vv[:, c0:c1], ALU.add, ALU.subtract)
                    for (c0, c1) in halves:
                        act(vv[:, c0:c1], vv[:, c0:c1], AF.Sqrt)
                    for (c0, c1) in halves:
                        recip(vv[:, c0:c1], vv[:, c0:c1])
                    for c in range(2):
                        z, th = zs[c], ths[c]
                        for (c0, c1) in halves:
                            tt("dve", z[:, c0:c1], cvf[:, c, c0:c1], mean[:, c0:c1], ALU.subtract)
                            tt("dve", z[:, c0:c1], z[:, c0:c1], vv[:, c0:c1], ALU.mult)
                        for (c0, c1) in halves:
                            act(th[:, c0:c1], z[:, c0:c1], AF.Tanh, scale=small[:, 80 + c:81 + c], bias=small[:, 82 + c:83 + c])
                        for (c0, c1) in halves:
                            ts("dve", ub[:, c0:c1], z[:, c0:c1], pvl(PV_LG + c, PV_LG + c + 1), pvl(PV_LB + c, PV_LB + c + 1), ALU.mult, ALU.add)
                            stt("dve", s2b[:, c, c0:c1], th[:, c0:c1], 1.0, ub[:, c0:c1], ALU.add, ALU.mult)
                    for oc in range(2):
                        bi = 0 + oc
                        for c in range(2):
                            mm(bk(bi, 0, 128, 0, n), wpw[:, c, oc * 128:(oc + 1) * 128], s2b[:, c, :n], c == 0, c == 1)
                        pwo = zs[oc]
                        act(pwo[:, :n], bk(bi, 0, 128, 0, n), AF.Identity, scale=0.5, bias=pvl(PV_BPW + oc, PV_BPW + oc + 1))
                        tt("dve", yT[:, 2 + oc, t0:t0 + n], pwo[:, :n], gts[oc][:, :n], ALU.mult)

            mark('L%dF' % l)
            if "F" in phases:
                A = Alloc(6 * T * 2)
                XT = A.get([128, 2, T], BF16, "XT")
                X1 = A.get([128, 18, 512], BF16, "X1")
                W1 = A.get([128, 2, 512], BF16, "W1")
                wf = A.get([128, 2, 256], BF16, "wf")
                dbuf = [A.get([128, 2048], BF16, "dft%d" % i) for i in range(3)]
                f32t = [A.get([128, 512], F32, "f%d" % i) for i in range(4)]
                dftC = A.get([128, 2, 2, 256], BF16, "dftC")
                for cs in range(2):
                    for j in range(2):
                        S.dma("sp", dftC[:, cs, j, :], dftC_d[cs, j * 128:(j + 1) * 128, :])
                S.dma("pool", wf[:, :, :], w_fo_d[l].rearrange("(k p) w -> p k w", p=128))
                for c in range(2):
                    mm(bk(0, 0, 128, 0, 256), cs128[:, 0:128], wf[:, c, :], True, True)
                    mm(bk(0, 0, 128, 256, 512), cs128[:, 128:256], wf[:, c, :], True, True)
                    cp("act", W1[:, c, :], bk(0, 0, 128, 0, 512))
                wfo = wload(w_in_d[l, :, C_FO:C_FO + 256])
                for tc, (t0, n) in enumerate(TCH):
                    for g in range(2):
                        bi = 1 + (tc * 2 + g) % 4
                        for k in range(8):
                            mm(bk(bi, 0, 128, 0, n), wfo[:, k, g * 128:(g + 1) * 128], hT[:, k, t0:t0 + n], k == 0, k == 7)
                        cp("act" if g else "dve", XT[:, g, t0:t0 + n], bk(bi, 0, 128, 0, n))
                for j in range(18):
                    bi = 5 + j % 3
                    for g in range(2):
                        mm(bk(bi, 0, 128, 0, 512), XT[:, g, j * 128:(j + 1) * 128], W1[:, g, :], g == 0, g == 1)
                    cp("act" if j % 2 else "dve", X1[:, j, :], bk(bi, 0, 128, 0, 512))
                di = 0
                for j in range(16):
                    for cs in range(2):
                        db = dbuf[di % 3]
                        S.dma("sp" if di % 2 == 0 else "act", db[:, :], dftL_d[cs, j * 128:(j + 1) * 128, :])
                        di += 1
                        for tcd in range(4):
                            for fc in range(2):
                                mm(bk(tcd * 2 + fc, 0, 128, 0, 512), X1[:, j, cs * 256 + fc * 128:cs * 256 + fc * 128 + 128],
                                   db[:, tcd * 512:(tcd + 1) * 512], j == 0 and cs == 0, j == 15 and cs == 1)
                wgt = wload(w_in_d[l, :, C_GATE + 512:C_GATE + 768])
                for tc, (t0, n) in enumerate(TCH):
                    if tc == 4:
                        for fc in range(2):
                            first = True
                            for j in range(2):
                                for cs in range(2):
                                    mm(bk(fc, 0, 128, 0, 256), X1[:, 16 + j, cs * 256 + fc * 128:cs * 256 + fc * 128 + 128],
                                       dftC[:, cs, j, :], first, j == 1 and cs == 1)
                                    first = False
                    for fc in range(2):
                        src_b = tc * 2 + fc if tc < 4 else fc
                        fo = f32t[fc]
                        act(fo[:, :n], bk(src_b, 0, 128, 0, n), AF.Identity, bias=pvl(PV_BFO + fc, PV_BFO + fc + 1))
                    for fc in range(2):
                        gb = tc * 2 + fc if tc < 4 else 2 + fc
                        s2 = f32t[2 + fc]
                        for k in range(8):
                            mm(bk(gb, 0, 128, 0, n), wgt[:, k, fc * 128:(fc + 1) * 128], hT[:, k, t0:t0 + n], k == 0, k == 7)
                        act(s2[:, :n], bk(gb, 0, 128, 0, n), AF.Tanh, scale=0.5)
                        stt("dve", s2[:, :n], s2[:, :n], 1.0, bk(gb, 0, 128, 0, n), ALU.add, ALU.mult)
                        tt("dve", yT[:, 4 + fc, t0:t0 + n], f32t[fc][:, :n], s2[:, :n], ALU.mult)

            mark('L%dG' % l)
            if "G" in phases:
                A = Alloc(E0)
                vt = A.get([128, 18, 256], F32, "vt")
                vln = A.get([128, 18, 256], BF16, "vln")
                wsT = A.get([128, 512], BF16, "wsT")
                bsf = A.get([128, 512], F32, "bsf")
                sgb = A.get([128, 512], F32, "sgb")
                sums = A.get([128, 18], F32, "sums")
                ssq = A.get([128, 18], F32, "ssq")
                junk = A.get([128, 256], BF16, "junk")
                mv = A.get([128, 2, 18], F32, "mv")
                memset("dve", sums[:, :], 0.0)
                memset("dve", ssq[:, :], 0.0)
                rs18 = A.get([128, 18], F32, "rs18")
                f32t = [A.get([128, 512], F32, "f%d" % i) for i in range(5)]
                S.dma("pool", wsT[:, :], w_sT_d[l])
                S.dma("sp", bsf[0:1, :], b_s_d[l])
                S.dma("sp", sgb[:, :], sgb_d[l])
                wv = wload(w_in_d[l, :, C_V:C_V + 256])
                for j in range(18):
                    bi = j % 4
                    for k in range(8):
                        mm(bk(bi, 0, 128, 0, 256), hT[:, k, j * 128:(j + 1) * 128], wv[:, k, :], k == 0, k == 7)
                    pb_ = bk(bi, 0, 128, 0, 256)
                    act(vt[:, j, :], pb_, AF.Identity)
                    act(vln[:, j, :], pb_, AF.Square)
                S.op("dve", lambda E: E.reduce_sum(sums[:, :].ap, vt[:, :, :].ap, axis=mybir.AxisListType.X), [vt[:, :, :]], [sums[:, :]])
                S.op("dve", lambda E: E.reduce_sum(ssq[:, :].ap, vln[:, :, :].ap, axis=mybir.AxisListType.X), [vln[:, :, :]], [ssq[:, :]])
                mean18 = mv[:, 0, :]
                ts("dve", mean18, sums[:, :], 1.0 / 256, None, ALU.mult)
                tt("dve", mv[:, 1, :], mean18, mean18, ALU.mult)
                veps = f32t[4][:, 0:256]
                memset("dve", veps, 1.0)
                stt("dve", veps[:, 0:18] if False else f32t[4][:, 0:18], ssq[:, :], 1.0 / 256, mv[:, 1, :], ALU.mult, ALU.subtract)
                ts("dve", f32t[4][:, 0:18], f32t[4][:, 0:18], 0.0, LN_EPS, ALU.max, ALU.add)
                act(f32t[3][:, 0:18], f32t[4][:, 0:18], AF.Sqrt)
                recip(rs18[:, :], f32t[3][:, 0:18])
                if dbg:
                    dd = f32t[2]
                    memset("dve", dd[:, 0:128], 0.0)
                    cp("dve", dd[:, 0:18], sums[:, :])
                    cp("dve", dd[:, 18:36], ssq[:, :])
                    cp("dve", dd[:, 36:54], mv[:, 0, :])
                    cp("dve", dd[:, 54:72], f32t[4][:, 0:18])
                    cp("dve", dd[:, 72:90], rs18[:, :])
                    cp("dve", dd[:, 90:108], f32t[3][:, 0:18])
                    S.dma("sp", dbg2_d, dd[:, 0:128])
                for j in range(18):
                    t_ = f32t[j % 2]
                    if "noln" in SKIP:
                        cp("dve", vln[:, j, :], vt[:, j, :])
                        continue
                    stt("dve", t_[:, 0:256], vt[:, j, :], mv[:, 0, j:j + 1], sgb[:, 0:256], ALU.subtract, ALU.mult)
                    stt("dve", vln[:, j, :], t_[:, 0:256], rs18[:, j:j + 1], sgb[:, 256:512], ALU.mult, ALU.add)
                wu = wload(w_in_d[l, :, C_U:C_U + 256])
                wgt = wload(w_in_d[l, :, C_GATE + 768:C_GATE + 1024])
                for tc, (t0, n) in enumerate(TCH):
                    for jj in range(n // 128):
                        j = t0 // 128 + jj
                        for g in range(4):
                            fc, r0 = g // 2, (g % 2) * 64
                            o = bk(fc, r0, r0 + 64, jj * 128, jj * 128 + 128)
                            if "nobias" in SKIP:
                                mm(o, vln[:, j, g * 64:(g + 1) * 64], wsT[:, g * 128:(g + 1) * 128], True, True)
                            else:
                                mm(o, vln[:, j, g * 64:(g + 1) * 64], wsT[:, g * 128:(g + 1) * 128], True, False)
                                mm(o, onesf[0:1, 0:64], bsf[0:1, g * 128:(g + 1) * 128], False, True)
                    for fc in range(2):
                        for k in range(8):
                            mm(bk(2 + fc, 0, 128, 0, n), wu[:, k, fc * 128:(fc + 1) * 128], hT[:, k, t0:t0 + n], k == 0, k == 7)
                        uf = f32t[0 + fc]
                        cp("act", uf[:, :n], bk(2 + fc, 0, 128, 0, n))
                        tt("dve", uf[:, :n], bk(fc, 0, 128, 0, n), uf[:, :n], ALU.mult)
                        s2 = f32t[2 + fc]
                        gate_s2(wgt, fc, t0, n, 4 + fc, f32t[4], s2[:, :n])
                        tt("dve", yT[:, 6 + fc, t0:t0 + n], uf[:, :n], s2[:, :n], ALU.mult)

            if dbg and l == nlayers - 1:
                dtmp = S.view(arena, E0, [128, 8, 512], F32, "dbgt")
                for g in range(8):
                    for tc, (t0, n) in enumerate(TCH):
                        cp("dve", dtmp[:, g, :n], yT[:, g, t0:t0 + n])
                        S.dma("sp", dbg_d[:, g * T + t0:g * T + t0 + n], dtmp[:, g, :n])

            mark('L%dH' % l)
            if "H" in phases:
                if "A" in phases and l + 1 < nlayers:
                    AH = Alloc(E0)
                    abufs = [AH.get([128, 8, 256], BF16, "abuf%d" % i) for i in range(4)]
                    nextA["gen"] = phaseA(l + 1, abufs)
                hcnt = 0
                for ocp in range(4):
                    wo = wload(w_out_d[l, :, ocp * 256:(ocp + 1) * 256])
                    for oo in range(2):
                        oc = ocp * 2 + oo
                        for tc, (t0, n) in enumerate(TCH):
                            s = 0 if tc < 4 else 1
                            bi = (oc * 5 + tc) % 8
                            for k in range(8):
                                mm(bk(bi, 0, 128, 0, n), wo[:, k, oo * 128:(oo + 1) * 128], yT[:, k, t0:t0 + n], k == 0, k == 7)
                            stt("dve", xT[:, oc, t0:t0 + n], bk(bi, 0, 128, 0, n), gh_c(oc, s), xT[:, oc, t0:t0 + n], ALU.mult, ALU.add)
                            hcnt += 1
                            if hcnt % 3 == 0:
                                stepA(1)

        mark('final')
        A = Alloc(E0)
        sqb = [A.get([128, 512], BF16, "sqb%d" % i) for i in range(3)]
        msb = [A.get([128, 512], F32, "ms%d" % i) for i in range(2)]
        rsb = [A.get([128, 512], F32, "rs%d" % i) for i in range(2)]
        ob = [A.get([128, 512], F32, "ob%d" % i) for i in range(4)]
        oi = 0
        for tc, (t0, n) in enumerate(TCH[:4]):
            bi = tc % 4
            for k in range(8):
                sq = sqb[k % 3]
                act(sq[:, :n], xT[:, k, t0:t0 + n], AF.Square)
                mm(bk(bi, 0, 128, 0, n), onesb[:, 0, :], sq[:, :n], k == 0, k == 7)
            ms, rs = msb[tc % 2], rsb[tc % 2]
            rstd_from(bk(bi, 0, 128, 0, n), n, RMS_EPS, ms, rs)
            for k in range(8):
                o = ob[oi % 4]
                oi += 1
                stt("dve", o[:, :n], xT[:, k, t0:t0 + n], pv[:, PV_FINAL + k:PV_FINAL + k + 1], rs[:, :n], ALU.mult, ALU.mult)
                S.dma("sp", outT_d[k * 128:(k + 1) * 128, t0:t0 + n], o[:, :n])
        S.finish()
        stats_ = S.emit()
        stats_['marks'] = marks
    return nc, stats_


def _fm(v):
    v = np.asarray(v, np.float32)
    return np.ascontiguousarray(v.reshape(-1, 128).T)


def _swap_idx():
    idx = []
    for r in range(32):
        axis, half, f = r // 16, (r % 16) // 8, r % 8
        idx.append(axis * 16 + (1 - half) * 8 + f)
    return idx


def _consts():
    t = np.arange(NLAT)
    row = (t // 64).astype(np.float64)
    col = (t % 64).astype(np.float64)
    inv = 10000.0 ** (-np.arange(0, 16, 2, dtype=np.float64) / 16)
    rope = np.zeros((2, 128, NLAT), np.float32)
    for r in range(32):
        axis, half, f = r // 16, (r % 16) // 8, r % 8
        pos = row if axis == 0 else col
        ang = (pos.astype(np.float32) * np.float32(inv[f])).astype(np.float64)
        rope[0, 64 + r] = np.cos(ang)
        rope[1, 64 + r] = np.sin(ang) * (-1.0 if half == 0 else 1.0)
    n = np.arange(NLAT, dtype=np.int64)
    ang = 2 * np.pi * ((n[:, None] * n[None, :]) % NLAT) / NLAT
    dftL = np.stack([np.cos(ang), -np.sin(ang)]) / math.sqrt(NLAT)
    m = np.arange(NCTX, dtype=np.int64)
    angc = 2 * np.pi * ((m[:, None] * m[None, :]) % NCTX) / NCTX
    dftC = np.stack([np.cos(angc), -np.sin(angc)]) / math.sqrt(NCTX)
    j = np.arange(64)
    a64 = 2 * np.pi * ((j[:, None] * j[None, :]) % 64) / 64
    cs = np.zeros((128, 256), np.float64)
    for b in range(2):
        cs[b * 64:(b + 1) * 64, b * 64:(b + 1) * 64] = np.cos(a64) / 8
        cs[b * 64:(b + 1) * 64, 128 + b * 64:128 + (b + 1) * 64] = np.sin(a64) / 8
    bf = ml_dtypes.bfloat16
    return dict(ropeT=rope, dftL=dftL.astype(np.float32).astype(bf), dftC=dftC.astype(np.float32).astype(bf),
                cs128=cs.astype(np.float32).astype(bf), ident=np.eye(128, dtype=np.float32).astype(bf))


def prep_inputs(inp):
    g = lambda k: np.asarray(inp[k], np.float32)
    L = DEPTH
    x, c, ctx, c_ctx = g("x"), g("c"), g("ctx"), g("c_ctx")
    w_in, w_uq, w_ukv = g("w_in"), g("w_uq"), g("w_ukv")
    sw = _swap_idx()
    perm_q = []
    for h in range(4):
        perm_q += [h * 96 + d for d in range(64)] + [h * 96 + 64 + s for s in sw]
    cols_rb = list(range(320, 384)) + [384 + s for s in sw]
    nope = [h * 128 + d for h in range(4) for d in range(64)]
    vv = [h * 128 + 64 + d for h in range(4) for d in range(64)]
    shared = dict(
        w_ada=g("w_ada"), w_in=w_in, w_ropeB=np.ascontiguousarray(w_in[:, :, cols_rb]),
        w_uq=w_uq, w_uqs=np.ascontiguousarray(w_uq[:, :, perm_q]),
        w_ukv2=np.ascontiguousarray(w_ukv[:, :, nope + vv]),
        w_pw=g("w_pw"), w_fourier=g("w_fourier"),
        w_sT=np.ascontiguousarray(g("w_s").transpose(0, 3, 1, 2).reshape(L, 128, 512)),
        b_s=np.ascontiguousarray(g("b_s").reshape(L, 1, 512)),
        sgu_gb=np.ascontiguousarray(np.concatenate([
            np.broadcast_to(g("sgu_ln_g")[:, None, :], (L, 128, 256)),
            np.broadcast_to(g("sgu_ln_b")[:, None, :], (L, 128, 256))], axis=2)),
        w_out=g("w_out"),
    )
    shared.update(_consts())
    pvs = []
    for l in range(L):
        cw = np.concatenate([g("conv_w")[l][:, cc * 128:(cc + 1) * 128].T for cc in range(2)], axis=1)
        pvs += [np.repeat(_fm(g("norm_g")[l]), 2, axis=1), np.repeat(_fm(g("b_ada")[l]), 2, axis=1),
                _fm(g("q_norm_g")[l]), _fm(g("kv_norm_g")[l]), _fm(g("conv_b")[l]), _fm(g("conv_ln_g")[l]),
                _fm(g("conv_ln_b")[l]), _fm(g("b_pw")[l]), _fm(g("b_fourier")[l]), cw]
    pvs.append(_fm(g("final_g")))
    pv_common = np.concatenate(pvs, axis=1)
    assert pv_common.shape[1] == PV_C, pv_common.shape
    in_maps = []
    for b in range(8):
        cp_ = np.stack([_fm(c[b]), _fm(c_ctx)], axis=2).reshape(128, 16)
        m = dict(shared)
        m["xT"] = np.ascontiguousarray(np.concatenate([x[b].T, ctx[b].T], axis=1))
        m["pv"] = np.ascontiguousarray(np.concatenate([pv_common, cp_], axis=1).astype(np.float32))
        in_maps.append(m)
    return in_maps


_NC_CACHE = {}


def kernel(**inputs):
    in_maps = prep_inputs(inputs)
    if "nc" not in _NC_CACHE:
        _NC_CACHE["nc"] = build(DEPTH)[0]
    nc = _NC_CACHE["nc"]
    res = run_bass_kernel_spmd(nc, in_maps, core_ids=list(range(8)))
    out = np.stack([np.ascontiguousarray(r["outT"].T) for r in res.results], axis=0)
    return out.astype(np.float32)
```

```python
import math
from contextlib import ExitStack
import numpy as np
import ml_dtypes
import concourse.bass as bass
import concourse.mybir as mybir
from concourse.bass_utils import run_bass_kernel_spmd

F32 = mybir.dt.float32
BF16 = mybir.dt.bfloat16
DTSIZE = {F32: 4, BF16: 2}
AF = mybir.ActivationFunctionType
ALU = mybir.AluOpType


class Ref:
    __slots__ = ("ap", "tile", "reg")

    def __init__(self, ap, tile, reg):
        self.ap, self.tile, self.reg = ap, tile, reg


class TileW:
    def __init__(self, S, name, shape, dtype, space, base=None, boff=0):
        self.S, self.name, self.shape, self.dtype, self.space = S, name, list(shape), dtype, space
        self.esz = DTSIZE[dtype]
        st = []
        acc = 1
        for d in reversed(self.shape[1:]):
            st.append(acc)
            acc *= d
        self.strides = list(reversed(st))
        self.free_elems = acc
        self.boff = boff
        if base is None:
            self.hist = []
            if space == "sbuf":
                self.t = S.stack.enter_context(S.nc.sbuf_tensor(name, self.shape, dtype))
            else:
                self.t = S.stack.enter_context(S.nc.psum_tensor(name, self.shape, dtype))
            self.apbase = None
        else:
            self.hist = base.hist
            assert boff % base.esz == 0 and boff % self.esz == 0
            nb = acc * self.esz
            assert boff + nb <= base.free_elems * base.esz, (name, boff, nb)
            a = base.t[:, boff // base.esz:(boff + nb) // base.esz]
            if dtype != base.dtype:
                a = a.bitcast(dtype)
            if len(self.shape) == 3:
                a = a.rearrange("p (a b) -> p a b", a=self.shape[1])
            elif len(self.shape) == 4:
                a = a.rearrange("p (a b c) -> p a b c", a=self.shape[1], b=self.shape[2])
            self.apbase = a

    def __getitem__(self, idx):
        if not isinstance(idx, tuple):
            idx = (idx,)
        idx = list(idx) + [slice(None)] * (len(self.shape) - len(idx))
        p = idx[0]
        if isinstance(p, int):
            p0, p1 = p, p + 1
        else:
            p0, p1, _ = p.indices(self.shape[0])
        lo = 0
        hi = 0
        for d, (ix, stv) in enumerate(zip(idx[1:], self.strides)):
            n = self.shape[d + 1]
            if isinstance(ix, int):
                lo += ix * stv
                hi += ix * stv
            else:
                a, b, s = ix.indices(n)
                cnt = max(0, (b - a + s - 1) // s)
                lo += a * stv
                hi += (a + (cnt - 1) * s) * stv
        if self.space == "psum":
            b0 = (lo * self.esz) // 2048 * 2048
            b1 = ((hi + 1) * self.esz + 2047) // 2048 * 2048
            reg = (0, 128, b0, b1)
        else:
            reg = (p0, p1, self.boff + lo * self.esz, self.boff + (hi + 1) * self.esz)
        src = self.t if self.apbase is None else self.apbase
        return Ref(src[tuple(idx)], self, reg)


def _overlap(a, b):
    return a[0] < b[1] and b[0] < a[1] and a[2] < b[3] and b[2] < a[3]


def _contains(a, b):
    return a[0] <= b[0] and a[1] >= b[1] and a[2] <= b[2] and a[3] >= b[3]


class Op:
    __slots__ = ("eng", "fn", "waits", "inc", "dma", "snap")

    def __init__(self, eng, fn):
        self.eng, self.fn, self.waits, self.inc, self.dma, self.snap = eng, fn, [], False, None, None


NDMASEM = 24
SKIP = set()


class Sched:
    def __init__(self, nc, stack):
        self.nc, self.stack = nc, stack
        self.engs = {"pe": nc.tensor, "act": nc.scalar, "dve": nc.vector, "pool": nc.gpsimd, "sp": nc.sync}
        self.ops = {e: [] for e in self.engs}
        self.vc = {e: {} for e in self.engs}
        self.dma_seq = [0] * NDMASEM
        self.dma_snap = [dict() for _ in range(NDMASEM)]
        self.dma_rr = 0

    def sb(self, name, shape, dtype):
        return TileW(self, name, shape, dtype, "sbuf")

    def ps(self, name, shape, dtype=F32):
        return TileW(self, name, shape, dtype, "psum")

    def view(self, base, boff, shape, dtype, name="v"):
        return TileW(self, name, shape, dtype, "sbuf", base=base, boff=boff)

    def _deps(self, reads, writes, eng=None):
        deps = set()
        for r in reads:
            ps = r.tile.space == "psum"
            for (reg, kind, e, i) in r.tile.hist:
                if (kind == "w" or (ps and e != eng)) and _overlap(reg, r.reg):
                    deps.add((e, i))
        for w in writes:
            for (reg, kind, e, i) in w.tile.hist:
                if _overlap(reg, w.reg):
                    deps.add((e, i))
        return deps

    def _snap_of(self, e, i):
        if e[0] == "q":
            return self.dma_snap[int(e[1:])][i]
        return self.ops[e][i].snap

    def _apply_waits(self, eng, deps):
        vc = self.vc[eng]
        waits = []
        for (e, i) in sorted(deps, key=lambda x: -x[1]):
            if e == eng and eng == "pe":
                continue
            if vc.get(e, -1) >= i:
                continue
            waits.append((e, i))
            for k, v in self._snap_of(e, i).items():
                if vc.get(k, -1) < v:
                    vc[k] = v
            if e[0] != "q":
                self.ops[e][i].inc = True
        return waits

    def _record(self, reads, writes, e, i):
        for r in reads:
            r.tile.hist.append((r.reg, "r", e, i))
        for w in writes:
            h = w.tile.hist
            h[:] = [x for x in h if not _contains(w.reg, x[0])]
            h.append((w.reg, "w", e, i))

    def op(self, eng, fn, reads=(), writes=()):
        reads = [r for r in reads if isinstance(r, Ref)]
        writes = [w for w in writes if isinstance(w, Ref)]
        deps = self._deps(reads, writes, eng)
        o = Op(eng, fn)
        o.waits = self._apply_waits(eng, deps)
        idx = len(self.ops[eng])
        o.snap = dict(self.vc[eng])
        o.snap[eng] = idx
        self.ops[eng].append(o)
        self._record(reads, writes, eng, idx)
        return o

    def dma(self, queue, out, in_):
        k = self.dma_rr
        self.dma_rr = (self.dma_rr + 1) % NDMASEM
        pe = "q%d" % k
        reads = [in_] if isinstance(in_, Ref) else []
        writes = [out] if isinstance(out, Ref) else []
        deps = self._deps(reads, writes)
        seq = self.dma_seq[k]
        if seq > 0:
            deps.add((pe, seq - 1))
        oap = out.ap if isinstance(out, Ref) else out
        iap = in_.ap if isinstance(in_, Ref) else in_

        def fn(E, oap=oap, iap=iap):
            return E.dma_start(out=oap, in_=iap)
        o = Op(queue, fn)
        o.waits = self._apply_waits(queue, deps)
        o.dma = (k, seq)
        idx = len(self.ops[queue])
        o.snap = dict(self.vc[queue])
        self.ops[queue].append(o)
        snap = dict(self.vc[queue])
        snap[pe] = seq
        self.dma_snap[k][seq] = snap
        self.dma_seq[k] = seq + 1
        self._record(reads, writes, pe, seq)
        return o

    def finish(self):
        deps = set()
        for k in range(NDMASEM):
            if self.dma_seq[k] > 0:
                deps.add(("q%d" % k, self.dma_seq[k] - 1))
        o = Op("sp", None)
        o.waits = self._apply_waits("sp", deps)
        self.ops["sp"].append(o)

    def emit(self):
        nc = self.nc
        sems = {e: self.stack.enter_context(nc.semaphore("s_" + e)) for e in self.engs}
        dsems = [self.stack.enter_context(nc.semaphore("d_%d" % k)) for k in range(NDMASEM)]
        rank = {}
        for e, ops in self.ops.items():
            c = 0
            r = []
            for o in ops:
                if o.inc:
                    c += 1
                r.append(c)
            rank[e] = r
        nwait = ninst = 0
        for e in ["sp", "pool", "pe", "act", "dve"]:
            E = self.engs[e]
            for o in self.ops[e]:
                for (e2, i2) in o.waits:
                    if e2[0] == "q":
                        E.wait_ge(dsems[int(e2[1:])], 16 * (i2 + 1))
                    else:
                        E.wait_ge(sems[e2], rank[e2][i2])
                    nwait += 1
                if o.fn is None:
                    continue
                inst = o.fn(E)
                ninst += 1
                if o.dma is not None:
                    inst.then_inc(dsems[o.dma[0]], 16)
                elif o.inc:
                    inst.then_inc(sems[e], 1)
        self.stats = dict(ninst=ninst, nwait=nwait, per_eng={e: len(v) for e, v in self.ops.items()})
        return self.stats


D = 1024
NLAT = 2048
NCTX = 256
T = NLAT + NCTX
DEPTH = 4
IN_DIM = 2720
TCH = [(0, 512), (512, 512), (1024, 512), (1536, 512), (2048, 256)]
SM_SCALE = 1.0 / math.sqrt(96.0)
RMS_EPS = 1e-6
LN_EPS = 1e-5
PV_NG, PV_BADA, PV_GQ, PV_GKV, PV_CB, PV_LG, PV_LB, PV_BPW, PV_BFO, PV_CW = 0, 16, 64, 66, 67, 69, 71, 73, 75, 77
NPL = 77 + 62
PV_FINAL = DEPTH * NPL
PV_C = PV_FINAL + 8
NPV = PV_C + 16
C_Q, C_KVR, C_CA, C_CG, C_FO, C_U, C_V, C_GATE = 0, 256, 416, 672, 928, 1184, 1440, 1696


def build(nlayers=DEPTH, dbg=False, phases="ABCDEFGH"):
    nc = bass.Bass("TRN2", target_bir_lowering=False)

    def din(name, shape, dt=F32):
        return nc.dram_tensor(name, list(shape), dt, kind="ExternalInput").ap()

    xT_d = din("xT", [D, T])
    pv_d = din("pv", [128, NPV])
    w_ada_d = din("w_ada", [DEPTH, D, 3 * D])
    w_in_d = din("w_in", [DEPTH, D, IN_DIM])
    w_ropeB_d = din("w_ropeB", [DEPTH, D, 96])
    w_uq_d = din("w_uq", [DEPTH, 256, 384])
    w_uqs_d = din("w_uqs", [DEPTH, 256, 384])
    w_ukv_d = din("w_ukv2", [DEPTH, 128, 512])
    w_pw_d = din("w_pw", [DEPTH, 256, 256])
    w_fo_d = din("w_fourier", [DEPTH, 256, 256])
    w_sT_d = din("w_sT", [DEPTH, 128, 512])
    b_s_d = din("b_s", [DEPTH, 1, 512])
    sgb_d = din("sgu_gb", [DEPTH, 128, 512])
    w_out_d = din("w_out", [DEPTH, D, D])
    rope_d = din("ropeT", [2, 128, NLAT])
    dftL_d = din("dftL", [2, NLAT, NLAT], BF16)
    dftC_d = din("dftC", [2, NCTX, NCTX], BF16)
    cs128_d = din("cs128", [128, 256], BF16)
    ident_d = din("ident", [128, 128], BF16)
    outT_d = nc.dram_tensor("outT", [D, NLAT], F32, kind="ExternalOutput").ap()
    dbg_d = None
    if dbg:
        dbg_d = nc.dram_tensor("dbgy", [128, 8 * T], F32, kind="ExternalOutput").ap()
        dbg2_d = nc.dram_tensor("dbg2", [128, 128], F32, kind="ExternalOutput").ap()

    with ExitStack() as st:
        S = Sched(nc, st)
        xT = S.sb("xTs", [128, 8, T], F32)
        hT = S.sb("hTs", [128, 8, T], BF16)
        ARENA_B = 81 * 1024
        arena = S.sb("arena", [128, ARENA_B // 2], BF16)
        wg = [S.sb("wg%d" % i, [128, 8, 256], BF16) for i in range(3)]
        pv = S.sb("pvs", [128, NPV], F32)
        smalls = [S.sb("small%d" % i, [128, 256], F32) for i in range(2)]
        th16t = S.sb("th16t", [128, 16], F32)
        cur = {"small": smalls[0]}
        scb = S.sb("scb", [128, 16], BF16)
        onesb = S.sb("onesb", [128, 4, 128], BF16)
        onesf = S.sb("onesf", [128, 64], F32)
        cm05 = S.sb("cm05", [128, 8], F32)
        ident = S.sb("identb", [128, 128], BF16)
        cs128 = S.sb("cs128s", [128, 256], BF16)
        PP = [S.ps("pp%d" % i, [128, 1024]) for i in range(4)]

        def bank(i):
            return PP[i // 2], (i % 2) * 512

        def bk(i, p0=0, p1=128, c0=0, c1=512):
            t, o = bank(i)
            return t[p0:p1, o + c0:o + c1]

        y_off = lambda g: g * T * 2
        yT = S.view(arena, 0, [128, 8, T], BF16, "yT")
        E0 = 8 * T * 2

        def mm(out, lhsT, rhs, start, stop):
            S.op("pe", lambda E: E.matmul(out.ap, lhsT.ap, rhs.ap, start=start, stop=stop), [lhsT, rhs], [out])

        def act(out, in_, func, scale=1.0, bias=0.0, eng="act"):
            sa = scale.ap if isinstance(scale, Ref) else scale
            ba = bias.ap if isinstance(bias, Ref) else bias
            S.op("act", lambda E: E.activation(out.ap, in_.ap, func, bias=ba, scale=sa), [in_, scale, bias], [out])

        def tt(eng, out, in0, in1, op):
            S.op(eng, lambda E: E.tensor_tensor(out.ap, in0.ap, in1.ap, op), [in0, in1], [out])

        def ts(eng, out, in0, s1, s2, op0, op1=None):
            a1 = s1.ap if isinstance(s1, Ref) else s1
            a2 = s2.ap if isinstance(s2, Ref) else s2
            if op1 is None:
                S.op(eng, lambda E: E.tensor_scalar(out.ap, in0.ap, a1, None, op0), [in0, s1], [out])
            else:
                S.op(eng, lambda E: E.tensor_scalar(out.ap, in0.ap, a1, a2, op0, op1), [in0, s1, s2], [out])

        def stt(eng, out, in0, sc, in1, op0, op1):
            a = sc.ap if isinstance(sc, Ref) else sc
            S.op(eng, lambda E: E.scalar_tensor_tensor(out.ap, in0.ap, a, in1.ap, op0, op1), [in0, sc, in1], [out])

        def cp(eng, out, in_):
            if eng == "act":
                act(out, in_, AF.Copy)
            else:
                S.op(eng, lambda E: E.tensor_copy(out.ap, in_.ap), [in_], [out])

        def memset(eng, out, val):
            S.op(eng, lambda E: E.memset(out.ap, val), [], [out])

        wg_rr = [0]
        marks = []

        def mark(name):
            marks.append((name, len(S.ops['pe']), len(S.ops['act']), len(S.ops['dve'])))

        def wload(src2d, width=256):
            b = wg[wg_rr[0] % 3]
            wg_rr[0] += 1
            S.dma("pool", b[:, :, 0:width], src2d.rearrange("(k p) w -> p k w", p=128))
            return b

        S.dma("sp", pv[:, :], pv_d)
        S.dma("sp", ident[:, :], ident_d)
        S.dma("sp", cs128[:, :], cs128_d)
        for k in range(8):
            S.dma("sp", xT[:, k, :], xT_d[k * 128:(k + 1) * 128, :])
        memset("pool", onesb[:, 0, :], 1.0 / 1024)
        memset("pool", onesb[:, 1, :], 1.0 / 256)
        memset("pool", onesb[:, 2, :], 1.0 / 128)
        memset("pool", onesb[:, 3, :], 1.0)
        memset("pool", onesf[:, :], 1.0)
        memset("pool", cm05[:, 0:1], RMS_EPS)
        memset("pool", cm05[:, 1:2], LN_EPS)
        cpk = pv[:, PV_C:PV_C + 16]
        th16 = th16t[:, :]
        act(th16, cpk, AF.Tanh, scale=0.5)
        stt("dve", scb[:, :], th16, 1.0, cpk, ALU.add, ALU.mult)

        def shift_c(k, s):
            return cur["small"][:, k * 2 + s:k * 2 + s + 1]

        def gs_c(k, s):
            return cur["small"][:, 48 + k * 2 + s:48 + k * 2 + s + 1]

        def gh_c(k, s):
            return cur["small"][:, 64 + k * 2 + s:64 + k * 2 + s + 1]

        def phaseA(l_, bufs=None):
            sm = smalls[l_ % 2]
            pvl_ = lambda c0, c1: pv[:, l_ * NPL + c0:l_ * NPL + c1]
            for jp in range(12):
                if bufs is None:
                    wb = wload(w_ada_d[l_, :, jp * 256:(jp + 1) * 256])
                else:
                    wb = bufs[jp % len(bufs)]
                    S.dma("pool", wb[:, :, :], w_ada_d[l_, :, jp * 256:(jp + 1) * 256].rearrange("(k p) w -> p k w", p=128))
                for jj in range(2):
                    for k in range(8):
                        mm(bk(7, 0, 128, jj * 2, jj * 2 + 2), wb[:, k, jj * 128:(jj + 1) * 128], scb[:, k * 2:k * 2 + 2], k == 0, k == 7)
                cp("dve", sm[:, 150 + jp * 4:150 + jp * 4 + 4], bk(7, 0, 128, 0, 4))
                yield
            stt("dve", sm[:, 0:48], sm[:, 150:198], 0.5, pvl_(PV_BADA, PV_BADA + 48), ALU.mult, ALU.add)
            stt("dve", sm[:, 48:64], sm[:, 16:32], 1.0, pvl_(PV_NG, PV_NG + 16), ALU.add, ALU.mult)
            ts("dve", sm[:, 64:80], sm[:, 32:48], 0.5, None, ALU.mult)
            ts("dve", sm[:, 80:82], pvl_(PV_LG, PV_LG + 2), 0.5, None, ALU.mult)
            ts("dve", sm[:, 82:84], pvl_(PV_LB, PV_LB + 2), 0.5, None, ALU.mult)
            ts("dve", sm[:, 84:146], pvl_(PV_CW, PV_CW + 62), 0.5, None, ALU.mult)
            yield

        nextA = {"gen": None}

        def stepA(nsteps=1):
            g_ = nextA["gen"]
            if g_ is None:
                return
            for _ in range(nsteps):
                try:
                    next(g_)
                except StopIteration:
                    nextA["gen"] = None
                    return

        class Alloc:
            def __init__(self, start):
                self.off = start

            def get(self, shape, dtype, name="t"):
                n = DTSIZE[dtype]
                for d in shape[1:]:
                    n *= d
                n = (n + 63) // 64 * 64
                v = S.view(arena, self.off, shape, dtype, name)
                self.off += n
                assert self.off <= ARENA_B, (name, self.off)
                return v

        def recip(out, in_):
            S.op("dve", lambda E: E.reciprocal(out.ap, in_.ap), [in_], [out])

        def epsc(eps):
            return cm05[:, 0:1] if eps == RMS_EPS else cm05[:, 1:2]

        def rstd_from(pb_ref, n, eps, ms, rstd):
            act(ms[:, :n], pb_ref, AF.Sqrt, bias=epsc(eps))
            recip(rstd[:, :n], ms[:, :n])

        def gate_s2(wgate, gl, t0, n, bi, th, s2out):
            for k in range(8):
                mm(bk(bi, 0, 128, 0, n), wgate[:, k, gl * 128:(gl + 1) * 128], hT[:, k, t0:t0 + n], k == 0, k == 7)
            act(th[:, :n], bk(bi, 0, 128, 0, n), AF.Tanh, scale=0.5)
            stt("dve", s2out, th[:, :n], 1.0, bk(bi, 0, 128, 0, n), ALU.add, ALU.mult)

        for l in range(nlayers):
            pvl = lambda c0, c1: pv[:, l * NPL + c0:l * NPL + c1]
            last = (l == DEPTH - 1) and not dbg
            TCL = TCH[:4] if last else TCH
            NTL = 16 if last else 18
            mark('L%dA' % l)
            if "A" in phases:
                if l == 0:
                    for _ in phaseA(0):
                        pass
                else:
                    stepA(100)
            cur["small"] = smalls[l % 2]
            small = cur["small"]

            mark('L%dB' % l)
            if "B" in phases:
                A = Alloc(E0)
                sqb = [A.get([128, 512], BF16, "sqb%d" % i) for i in range(3)]
                msb = [A.get([128, 512], F32, "ms%d" % i) for i in range(2)]
                rsb = [A.get([128, 512], F32, "rs%d" % i) for i in range(2)]
                tmpb = [A.get([128, 512], F32, "tmp%d" % i) for i in range(3)]
                def sq_mm(tc_):
                    t0_, n_ = TCH[tc_]
                    for k in range(8):
                        sq = sqb[k % 3]
                        act(sq[:, :n_], xT[:, k, t0_:t0_ + n_], AF.Square)
                        mm(bk(tc_ % 4, 0, 128, 0, n_), onesb[:, 0, :], sq[:, :n_], k == 0, k == 7)

                sq_mm(0)
                for tc, (t0, n) in enumerate(TCH):
                    s = 0 if tc < 4 else 1
                    bi = tc % 4
                    ms, rs = msb[tc % 2], rsb[tc % 2]
                    rstd_from(bk(bi, 0, 128, 0, n), n, RMS_EPS, ms, rs)
                    if tc + 1 < len(TCH):
                        sq_mm(tc + 1)
                    for k in range(8):
                        tmp = tmpb[k % 3]
                        stt("dve", tmp[:, :n], xT[:, k, t0:t0 + n], gs_c(k, s), rs[:, :n], ALU.mult, ALU.mult)
                        act(hT[:, k, t0:t0 + n], tmp[:, :n], AF.Identity, bias=shift_c(k, s))

            mark('L%dC' % l)
            if "C" in phases:
                A = Alloc(2 * T * 2)
                kT = A.get([128, 4, T], BF16, "kT")
                Vt4 = A.get([128, 18, 4, 128], BF16, "Vt4")
                tab = A.get([128, 2, 512], F32, "tab")
                f32t = [A.get([128, 512], F32, "f%d" % i) for i in range(4)]
                dbase = A.off
                NPT = 3
                PT = [A.get([128, 1024], BF16, "PT%d" % i) for i in range(NPT)]
                cqT = A.get([128, 2, 512], BF16, "cqT")
                s2g = A.get([128, 2, 512], F32, "s2g")
                qT = A.get([128, 2, 4, 512], BF16, "qT")
                wsm = A.get([128, 2, 384], BF16, "wuq")
                wsms = A.get([128, 2, 384], BF16, "wuqs")
                A2 = Alloc(dbase)
                ckvT = A2.get([128, T], BF16, "ckvT")
                wukv = A2.get([128, 512], BF16, "wukv")
                wrB = A2.get([128, 8, 96], BF16, "wrB")
                assert A2.off <= dbase + NPT * 2048 + 2048
                A3 = Alloc(dbase + NPT * 2048 + 2048)
                sqbC = [A3.get([128, 512], BF16, "sqbC%d" % i) for i in range(2)]
                t2C = A3.get([128, 512], F32, "t2C")
                sqb = PT
                memset("dve", kT[:, :, :], 0.0)
                memset("dve", qT[:, :, :, :], 0.0)
                for j in range(18):
                    memset("dve", Vt4[:, j, :, 64:128], 1.0)
                S.dma("pool", wsm[:, :, :], w_uq_d[l].rearrange("(k p) w -> p k w", p=128))
                S.dma("pool", wsms[:, :, :], w_uqs_d[l].rearrange("(k p) w -> p k w", p=128))
                S.dma("pool", wukv[:, :], w_ukv_d[l])
                S.dma("pool", wrB[:, :, :], w_ropeB_d[l].rearrange("(k p) w -> p k w", p=128))
                wkvr = wload(w_in_d[l, :, C_KVR:C_KVR + 256])
                gkv = pvl(PV_GKV, PV_GKV + 1)
                b_kv, b_a, b_b, b_n = 0, 1, 2, 3

                def kv_mm(tc_):
                    t0_, n_ = TCH[tc_]
                    for k in range(8):
                        mm(bk(b_kv, 0, 128, 0, n_), wkvr[:, k, 0:128], hT[:, k, t0_:t0_ + n_], k == 0, k == 7)
                    for k in range(8):
                        mm(bk(b_a, 0, 96, 0, n_), wkvr[:, k, 64:160], hT[:, k, t0_:t0_ + n_], k == 0, k == 7)
                    if tc_ < 4:
                        for k in range(8):
                            mm(bk(b_b, 0, 96, 0, n_), wrB[:, k, :], hT[:, k, t0_:t0_ + n_], k == 0, k == 7)

                kv_mm(0)
                for tc, (t0, n) in enumerate(TCH):
                    lat = tc < 4
                    tb = tab
                    if lat:
                        S.dma("sp", tb[64:96, 0, :], rope_d[0, 64:96, t0:t0 + 512])
                        S.dma("sp", tb[64:96, 1, :], rope_d[1, 64:96, t0:t0 + 512])
                    sq = sqbC[tc % 2]
                    act(sq[:, :n], bk(b_kv, 0, 128, 0, n), AF.Square)
                    kvg = f32t[0]
                    act(kvg[:, :n], bk(b_kv, 0, 128, 0, n), AF.Copy, scale=gkv)
                    if lat:
                        t1, t2 = f32t[3], t2C
                        tt("dve", t1[64:96, :n], bk(b_a, 64, 96, 0, n), tb[64:96, 0, :n], ALU.mult)
                        tt("dve", t2[64:96, :n], bk(b_b, 64, 96, 0, n), tb[64:96, 1, :n], ALU.mult)
                        tt("pool", kT[64:96, 0, t0:t0 + n], t1[64:96, :n], t2[64:96, :n], ALU.add)
                        for h in range(1, 4):
                            cp("pool", kT[64:96, h, t0:t0 + n], kT[64:96, 0, t0:t0 + n])
                    else:
                        for h in range(4):
                            cp("act" if h % 2 else "dve", kT[64:96, h, t0:t0 + n], bk(b_a, 64, 96, 0, n))
                    mm(bk(b_n, 0, 128, 0, n), onesb[:, 2, :], sq[:, :n], True, True)
                    if tc + 1 < len(TCH):
                        kv_mm(tc + 1)
                    rstd_from(bk(b_n, 0, 128, 0, n), n, RMS_EPS, f32t[1], f32t[2])
                    tt("dve", ckvT[:, t0:t0 + n], kvg[:, :n], f32t[2][:, :n], ALU.mult)
                    for h in range(0 if "knope" not in SKIP else 4, 4):
                        bi = 4 + (h % 2)
                        mm(bk(bi, 0, 64, 0, n), wukv[:, h * 64:(h + 1) * 64], ckvT[:, t0:t0 + n], True, True)
                        cp("act", kT[0:64, h, t0:t0 + n], bk(bi, 0, 64, 0, n))
                    for jj in range(n // 128 if "v" not in SKIP else 0):
                        j = t0 // 128 + jj
                        bi = 6 + (jj // 2) % 2
                        pbv = bk(bi, 0, 128, (jj % 2) * 256, (jj % 2) * 256 + 256)
                        mm(pbv, ckvT[:, j * 128:(j + 1) * 128], wukv[:, 256:512], True, True)
                        pbv3 = Ref(pbv.ap.rearrange("p (b c) -> p b c", b=4), pbv.tile, pbv.reg)
                        cp("act", Vt4[:, j, :, 0:64], pbv3)
                mark('L%dD' % l)
                if "D" in phases:
                    wq = wload(w_in_d[l, :, C_Q:C_Q + 256])
                    wgt = wload(w_in_d[l, :, C_GATE:C_GATE + 256])
                    gq = lambda g: pvl(PV_GQ + g, PV_GQ + g + 1)
                    pt_rr = 0
                    for qc, (t0, n) in enumerate(TCL):
                        lat = qc < 4
                        qb = qc % 2
                        if lat:
                            tb = tab
                            S.dma("sp", tb[64:96, 0, :], rope_d[0, 64:96, t0:t0 + 512])
                            S.dma("sp", tb[64:96, 1, :], rope_d[1, 64:96, t0:t0 + 512])
                        for g in range(2):
                            for k in range(8):
                                mm(bk(g, 0, 128, 0, n), wq[:, k, g * 128:(g + 1) * 128], hT[:, k, t0:t0 + n], k == 0, k == 7)
                        for g in range(2):
                            act(sqb[g][:, :n], bk(g, 0, 128, 0, n), AF.Square)
                            ts("dve", f32t[g][:, :n], bk(g, 0, 128, 0, n), gq(g), None, ALU.mult)
                            mm(bk(2, 0, 128, 0, n), onesb[:, 1, :], sqb[g][:, :n], g == 0, g == 1)
                        for g in range(2):
                            pbg = bk(4 + g, 0, 128, 0, n)
                            for k in range(8):
                                mm(pbg, wgt[:, k, g * 128:(g + 1) * 128], hT[:, k, t0:t0 + n], k == 0, k == 7)
                        rstd_from(bk(2, 0, 128, 0, n), n, RMS_EPS, f32t[2], f32t[3])
                        for g in range(2):
                            tt("dve", cqT[:, g, :n], f32t[g][:, :n], f32t[3][:, :n], ALU.mult)
                        for g in range(2):
                            pbg = bk(4 + g, 0, 128, 0, n)
                            act(s2g[:, g, :n], pbg, AF.Tanh, scale=0.5)
                            stt("dve", s2g[:, g, :n], s2g[:, g, :n], 1.0, pbg, ALU.add, ALU.mult)
                        for h in range(4):
                            ba, bb = 0 + (h % 2) * 2, 1 + (h % 2) * 2
                            for g in range(2):
                                mm(bk(ba, 0, 96, 0, n), wsm[:, g, h * 96:(h + 1) * 96], cqT[:, g, :n], g == 0, g == 1)
                            if lat:
                                for g in range(2):
                                    mm(bk(bb, 0, 96, 0, n), wsms[:, g, h * 96:(h + 1) * 96], cqT[:, g, :n], g == 0, g == 1)
                            cp("act", qT[0:64, qb, h, :n], bk(ba, 0, 64, 0, n))
                            if lat:
                                t1, t2 = f32t[0], f32t[1]
                                tt("dve", t1[64:96, :n], bk(ba, 64, 96, 0, n), tb[64:96, 0, :n], ALU.mult)
                                tt("dve", t2[64:96, :n], bk(bb, 64, 96, 0, n), tb[64:96, 1, :n], ALU.mult)
                                tt("pool", qT[64:96, qb, h, :n], t1[64:96, :n], t2[64:96, :n], ALU.add)
                            else:
                                cp("dve", qT[64:96, qb, h, :n], bk(ba, 64, 96, 0, n))
                        kts = list(range(18)) if lat else [16, 17]
                        npair = len(kts) // 2
                        items = [(h, pi) for h in range(4) for pi in range(npair)]
                        LA = 2

                        def issue_S(ii):
                            h_, pi_ = items[ii]
                            sp_ = PP[ii % 3]
                            k0_, k1_ = kts[2 * pi_], kts[2 * pi_ + 1]
                            mm(sp_[:, 0:n], kT[:, h_, k0_ * 128:(k0_ + 1) * 128], qT[:, qb, h_, :n], True, True)
                            mm(sp_[:, 512:512 + n], kT[:, h_, k1_ * 128:(k1_ + 1) * 128], qT[:, qb, h_, :n], True, True)

                        for ii in range(min(LA, len(items))):
                            issue_S(ii)
                        for ii, (h, pi) in enumerate(items):
                            if ii + LA < len(items):
                                issue_S(ii + LA)
                            sp_t = PP[ii % 3]
                            P = PT[ii % 3]
                            ob = 6 + (h % 2)
                            if n == 512:
                                act(P[:, 0:1024], sp_t[:, 0:1024], AF.Exp, scale=SM_SCALE)
                            else:
                                act(P[:, 0:n], sp_t[:, 0:n], AF.Exp, scale=SM_SCALE)
                                act(P[:, 512:512 + n], sp_t[:, 512:512 + n], AF.Exp, scale=SM_SCALE)
                            for jj, kt in enumerate((kts[2 * pi], kts[2 * pi + 1])):
                                first = (pi == 0 and jj == 0)
                                last = (pi == npair - 1 and jj == 1)
                                mm(bk(ob, 0, 128, 0, n), Vt4[:, kt, h, :], P[:, jj * 512:jj * 512 + n], first, last)
                            if pi == npair - 1:
                                rden = f32t[2]
                                recip(rden[64:128, :n], bk(ob, 64, 128, 0, n))
                                r0 = (h % 2) * 64
                                on = f32t[h % 2]
                                tt("dve", on[r0:r0 + 64, :n], bk(ob, 0, 64, 0, n), rden[64:128, :n], ALU.mult)
                                tt("pool", yT[r0:r0 + 64, h // 2, t0:t0 + n], on[r0:r0 + 64, :n], s2g[r0:r0 + 64, h // 2, :n], ALU.mult)

            mark('L%dE' % l)
            if "E" in phases:
                A = Alloc(4 * T * 2)
                glu = A.get([128, 2, 2364], BF16, "glu")
                dg = A.get([128, 2, 31, 128], BF16, "dg")
                wpw = A.get([128, 2, 256], BF16, "wpw")
                cvfs = [A.get([128, 2, 512], F32, "cvf%d" % i) for i in range(2)]
                cvbs = [A.get([128, 2, 512], BF16, "cvb%d" % i) for i in range(2)]
                sq2s = [A.get([128, 2, 512], BF16, "sq2%d" % i) for i in range(2)]
                gts = [A.get([128, 512], F32, "gts%d" % i) for i in range(2)]
                s2b = A.get([128, 2, 512], BF16, "s2b")
                f32t = [A.get([128, 512], F32, "f%d" % i) for i in range(7)]
                S.dma("pool", wpw[:, :, :], w_pw_d[l].rearrange("(k p) w -> p k w", p=128))
                GOFF = [15, 15 + 2048 + 30]
                for c in range(2):
                    memset("pool", glu[:, c, 0:15], 0.0)
                    memset("pool", glu[:, c, 15 + 2048:15 + 2048 + 30], 0.0)
                    memset("pool", glu[:, c, 2364 - 15:2364], 0.0)
                dgi = [0]
                wca = wload(w_in_d[l, :, C_CA:C_CA + 256])
                wcg = wload(w_in_d[l, :, C_CG:C_CG + 256])
                for tc, (t0, n) in enumerate(TCL):
                    po = GOFF[0] + t0 if tc < 4 else GOFF[1]
                    for c in range(2):
                        ba, bg = (c * 2) % 4, (c * 2 + 1) % 4
                        for k in range(8):
                            mm(bk(ba, 0, 128, 0, n), wca[:, k, c * 128:(c + 1) * 128], hT[:, k, t0:t0 + n], k == 0, k == 7)
                        for k in range(8):
                            mm(bk(bg, 0, 128, 0, n), wcg[:, k, c * 128:(c + 1) * 128], hT[:, k, t0:t0 + n], k == 0, k == 7)
                        th = f32t[c]
                        act(th[:, :n], bk(bg, 0, 128, 0, n), AF.Tanh, scale=0.5)
                        stt("dve", glu[:, c, po:po + n], th[:, :n], 1.0, bk(ba, 0, 128, 0, n), ALU.add, ALU.mult)
                        for _ in range(8):
                            if dgi[0] < 62:
                                c_, k_ = divmod(dgi[0], 31)
                                ts("dve", dg[:, c_, k_, :], ident[:, :], small[:, 84 + c_ * 31 + k_:84 + c_ * 31 + k_ + 1], None, ALU.mult)
                                dgi[0] += 1
                wgt = wload(w_in_d[l, :, C_GATE + 256:C_GATE + 512])
                def conv_mm(tc_):
                    t0_, n_ = TCH[tc_]
                    po_ = t0_ if tc_ < 4 else GOFF[1] - 15
                    for c in range(2):
                        for k in range(31):
                            mm(bk(4 + c, 0, 128, 0, n_), dg[:, c, k, :], glu[:, c, po_ + k:po_ + k + n_], k == 0, k == 30)

                conv_mm(0)
                for tc, (t0, n) in enumerate(TCL):
                    db_ = tc % 2
                    cvf, cvb, sq2 = cvfs[db_], cvbs[db_], sq2s[db_]
                    for c in range(2):
                        bi = 4 + c
                        cb = pvl(PV_CB + c, PV_CB + c + 1)
                        act(cvf[:, c, :n], bk(bi, 0, 128, 0, n), AF.Identity, bias=cb)
                        act(sq2[:, c, :n], bk(bi, 0, 128, 0, n), AF.Square, bias=cb)
                        cp("dve", cvb[:, c, :n], cvf[:, c, :n])
                    for c in range(2):
                        mm(bk(6, 0, 128, 0, n), onesb[:, 1, :], cvb[:, c, :n], c == 0, c == 1)
                    for c in range(2):
                        mm(bk(7, 0, 128, 0, n), onesb[:, 1, :], sq2[:, c, :n], c == 0, c == 1)
                    for oc in range(2):
                        s2 = gts[oc]
                        for k in range(8):
                            mm(bk(2 + oc, 0, 128, 0, n), wgt[:, k, oc * 128:(oc + 1) * 128], hT[:, k, t0:t0 + n], k == 0, k == 7)
                        act(s2[:, :n], bk(2 + oc, 0, 128, 0, n), AF.Tanh, scale=0.5)
                        stt("dve", s2[:, :n], s2[:, :n], 1.0, bk(2 + oc, 0, 128, 0, n), ALU.add, ALU.mult)
                    if tc + 1 < len(TCL):
                        conv_mm(tc + 1)
                    mean, m2, var, rstd = f32t[0], f32t[1], f32t[2], f32t[3]
                    cp("act", mean[:, :n], bk(6, 0, 128, 0, n))
                    tt("pool", m2[:, :n], mean[:, :n], mean[:, :n], ALU.mult)
                    stt("dve", var[:, :n], bk(7, 0, 128, 0, n), LN_EPS, m2[:, :n], ALU.add, ALU.subtract)
                    act(m2[:, :n], var[:, :n], AF.Sqrt)
                    recip(rstd[:, :n], m2[:, :n])
                    for c in range(2):
                        d_, z, th = f32t[4], f32t[5], f32t[6]
                        tt("dve", d_[:, :n], cvf[:, c, :n], mean[:, :n], ALU.subtract)
                        tt("dve", z[:, :n], d_[:, :n], rstd[:, :n], ALU.mult)
                        act(th[:, :n], z[:, :n], AF.Tanh, scale=small[:, 80 + c:81 + c], bias=small[:, 82 + c:83 + c])
                        ts("dve", d_[:, :n], z[:, :n], pvl(PV_LG + c, PV_LG + c + 1), pvl(PV_LB + c, PV_LB + c + 1), ALU.mult, ALU.add)
                        stt("dve", s2b[:, c, :n], th[:, :n], 1.0, d_[:, :n], ALU.add, ALU.mult)
                    for oc in range(2):
                        bi = 0 + oc
                        for c in range(2):
                            mm(bk(bi, 0, 128, 0, n), wpw[:, c, oc * 128:(oc + 1) * 128], s2b[:, c, :n], c == 0, c == 1)
                        pwo = f32t[4 + oc]
                        act(pwo[:, :n], bk(bi, 0, 128, 0, n), AF.Identity, scale=0.5, bias=pvl(PV_BPW + oc, PV_BPW + oc + 1))
                        tt("dve", yT[:, 2 + oc, t0:t0 + n], pwo[:, :n], gts[oc][:, :n], ALU.mult)

            mark('L%dF' % l)
            if "F" in phases:
                A = Alloc(6 * T * 2)
                XT = A.get([128, 2, T], BF16, "XT")
                X1 = A.get([128, 18, 512], BF16, "X1")
                W1 = A.get([128, 2, 512], BF16, "W1")
                wf = A.get([128, 2, 256], BF16, "wf")
                dbuf = [A.get([128, 2048], BF16, "dft%d" % i) for i in range(3)]
                f32t = [A.get([128, 512], F32, "f%d" % i) for i in range(4)]
                dftC = A.get([128, 2, 2, 256], BF16, "dftC")
                for cs in range(2):
                    for j in range(2):
                        S.dma("sp", dftC[:, cs, j, :], dftC_d[cs, j * 128:(j + 1) * 128, :])
                S.dma("pool", wf[:, :, :], w_fo_d[l].rearrange("(k p) w -> p k w", p=128))
                for c in range(2):
                    mm(bk(0, 0, 128, 0, 256), cs128[:, 0:128], wf[:, c, :], True, True)
                    mm(bk(0, 0, 128, 256, 512), cs128[:, 128:256], wf[:, c, :], True, True)
                    cp("act", W1[:, c, :], bk(0, 0, 128, 0, 512))
                wfo = wload(w_in_d[l, :, C_FO:C_FO + 256])
                for tc, (t0, n) in enumerate(TCL):
                    for g in range(2):
                        bi = 1 + (tc * 2 + g) % 4
                        for k in range(8):
                            mm(bk(bi, 0, 128, 0, n), wfo[:, k, g * 128:(g + 1) * 128], hT[:, k, t0:t0 + n], k == 0, k == 7)
                        cp("act" if g else "dve", XT[:, g, t0:t0 + n], bk(bi, 0, 128, 0, n))
                for j in range(NTL):
                    bi = 5 + j % 3
                    for g in range(2):
                        mm(bk(bi, 0, 128, 0, 512), XT[:, g, j * 128:(j + 1) * 128], W1[:, g, :], g == 0, g == 1)
                    cp("act" if j % 2 else "dve", X1[:, j, :], bk(bi, 0, 128, 0, 512))
                di = 0
                for j in range(16):
                    for cs in range(2):
                        db = dbuf[di % 3]
                        S.dma("sp" if di % 2 == 0 else "act", db[:, :], dftL_d[cs, j * 128:(j + 1) * 128, :])
                        di += 1
                        for tcd in range(4):
                            for fc in range(2):
                                mm(bk(tcd * 2 + fc, 0, 128, 0, 512), X1[:, j, cs * 256 + fc * 128:cs * 256 + fc * 128 + 128],
                                   db[:, tcd * 512:(tcd + 1) * 512], j == 0 and cs == 0, j == 15 and cs == 1)
                wgt = wload(w_in_d[l, :, C_GATE + 512:C_GATE + 768])
                for tc, (t0, n) in enumerate(TCL):
                    if tc == 4:
                        for fc in range(2):
                            first = True
                            for j in range(2):
                                for cs in range(2):
                                    mm(bk(fc, 0, 128, 0, 256), X1[:, 16 + j, cs * 256 + fc * 128:cs * 256 + fc * 128 + 128],
                                       dftC[:, cs, j, :], first, j == 1 and cs == 1)
                                    first = False
                    for fc in range(2):
                        src_b = tc * 2 + fc if tc < 4 else fc
                        fo = f32t[fc]
                        act(fo[:, :n], bk(src_b, 0, 128, 0, n), AF.Identity, bias=pvl(PV_BFO + fc, PV_BFO + fc + 1))
                    for fc in range(2):
                        gb = tc * 2 + fc if tc < 4 else 2 + fc
                        s2 = f32t[2 + fc]
                        for k in range(8):
                            mm(bk(gb, 0, 128, 0, n), wgt[:, k, fc * 128:(fc + 1) * 128], hT[:, k, t0:t0 + n], k == 0, k == 7)
                        act(s2[:, :n], bk(gb, 0, 128, 0, n), AF.Tanh, scale=0.5)
                        stt("dve", s2[:, :n], s2[:, :n], 1.0, bk(gb, 0, 128, 0, n), ALU.add, ALU.mult)
                        tt("dve", yT[:, 4 + fc, t0:t0 + n], f32t[fc][:, :n], s2[:, :n], ALU.mult)

            mark('L%dG' % l)
            if "G" in phases:
                A = Alloc(E0)
                vt = A.get([128, 18, 256], F32, "vt")
                vln = A.get([128, 18, 256], BF16, "vln")
                wsT = A.get([128, 512], BF16, "wsT")
                bsf = A.get([128, 512], F32, "bsf")
                sgb = A.get([128, 512], F32, "sgb")
                sums = A.get([128, 18], F32, "sums")
                ssq = A.get([128, 18], F32, "ssq")
                junk = A.get([128, 256], BF16, "junk")
                mv = A.get([128, 2, 18], F32, "mv")
                memset("dve", sums[:, :], 0.0)
                memset("dve", ssq[:, :], 0.0)
                rs18 = A.get([128, 18], F32, "rs18")
                f32t = [A.get([128, 512], F32, "f%d" % i) for i in range(5)]
                S.dma("pool", wsT[:, :], w_sT_d[l])
                S.dma("sp", bsf[0:1, :], b_s_d[l])
                S.dma("sp", sgb[:, :], sgb_d[l])
                wv = wload(w_in_d[l, :, C_V:C_V + 256])
                for j in range(NTL):
                    bi = j % 4
                    for k in range(8):
                        mm(bk(bi, 0, 128, 0, 256), hT[:, k, j * 128:(j + 1) * 128], wv[:, k, :], k == 0, k == 7)
                    pb_ = bk(bi, 0, 128, 0, 256)
                    act(vt[:, j, :], pb_, AF.Identity)
                    act(vln[:, j, :], pb_, AF.Square)
                S.op("dve", lambda E: E.reduce_sum(sums[:, :].ap, vt[:, :, :].ap, axis=mybir.AxisListType.X), [vt[:, :, :]], [sums[:, :]])
                S.op("dve", lambda E: E.reduce_sum(ssq[:, :].ap, vln[:, :, :].ap, axis=mybir.AxisListType.X), [vln[:, :, :]], [ssq[:, :]])
                mean18 = mv[:, 0, :]
                ts("dve", mean18, sums[:, :], 1.0 / 256, None, ALU.mult)
                tt("dve", mv[:, 1, :], mean18, mean18, ALU.mult)
                veps = f32t[4][:, 0:256]
                memset("dve", veps, 1.0)
                stt("dve", veps[:, 0:18] if False else f32t[4][:, 0:18], ssq[:, :], 1.0 / 256, mv[:, 1, :], ALU.mult, ALU.subtract)
                ts("dve", f32t[4][:, 0:18], f32t[4][:, 0:18], 0.0, LN_EPS, ALU.max, ALU.add)
                act(f32t[3][:, 0:18], f32t[4][:, 0:18], AF.Sqrt)
                recip(rs18[:, :], f32t[3][:, 0:18])
                if dbg:
                    dd = f32t[2]
                    memset("dve", dd[:, 0:128], 0.0)
                    cp("dve", dd[:, 0:18], sums[:, :])
                    cp("dve", dd[:, 18:36], ssq[:, :])
                    cp("dve", dd[:, 36:54], mv[:, 0, :])
                    cp("dve", dd[:, 54:72], f32t[4][:, 0:18])
                    cp("dve", dd[:, 72:90], rs18[:, :])
                    cp("dve", dd[:, 90:108], f32t[3][:, 0:18])
                    S.dma("sp", dbg2_d, dd[:, 0:128])
                for j in range(NTL):
                    t_ = f32t[j % 2]
                    if "noln" in SKIP:
                        cp("dve", vln[:, j, :], vt[:, j, :])
                        continue
                    stt("dve", t_[:, 0:256], vt[:, j, :], mv[:, 0, j:j + 1], sgb[:, 0:256], ALU.subtract, ALU.mult)
                    stt("dve", vln[:, j, :], t_[:, 0:256], rs18[:, j:j + 1], sgb[:, 256:512], ALU.mult, ALU.add)
                wu = wload(w_in_d[l, :, C_U:C_U + 256])
                wgt = wload(w_in_d[l, :, C_GATE + 768:C_GATE + 1024])
                for tc, (t0, n) in enumerate(TCL):
                    for jj in range(n // 128):
                        j = t0 // 128 + jj
                        for g in range(4):
                            fc, r0 = g // 2, (g % 2) * 64
                            o = bk(fc, r0, r0 + 64, jj * 128, jj * 128 + 128)
                            if "nobias" in SKIP:
                                mm(o, vln[:, j, g * 64:(g + 1) * 64], wsT[:, g * 128:(g + 1) * 128], True, True)
                            else:
                                mm(o, vln[:, j, g * 64:(g + 1) * 64], wsT[:, g * 128:(g + 1) * 128], True, False)
                                mm(o, onesf[0:1, 0:64], bsf[0:1, g * 128:(g + 1) * 128], False, True)
                    for fc in range(2):
                        for k in range(8):
                            mm(bk(2 + fc, 0, 128, 0, n), wu[:, k, fc * 128:(fc + 1) * 128], hT[:, k, t0:t0 + n], k == 0, k == 7)
                        uf = f32t[0 + fc]
                        cp("act", uf[:, :n], bk(2 + fc, 0, 128, 0, n))
                        tt("dve", uf[:, :n], bk(fc, 0, 128, 0, n), uf[:, :n], ALU.mult)
                        s2 = f32t[2 + fc]
                        gate_s2(wgt, fc, t0, n, 4 + fc, f32t[4], s2[:, :n])
                        tt("dve", yT[:, 6 + fc, t0:t0 + n], uf[:, :n], s2[:, :n], ALU.mult)

            if dbg and l == nlayers - 1:
                dtmp = S.view(arena, E0, [128, 8, 512], F32, "dbgt")
                for g in range(8):
                    for tc, (t0, n) in enumerate(TCL):
                        cp("dve", dtmp[:, g, :n], yT[:, g, t0:t0 + n])
                        S.dma("sp", dbg_d[:, g * T + t0:g * T + t0 + n], dtmp[:, g, :n])

            mark('L%dH' % l)
            if "H" in phases:
                if "A" in phases and l + 1 < nlayers:
                    AH = Alloc(E0)
                    abufs = [AH.get([128, 8, 256], BF16, "abuf%d" % i) for i in range(4)]
                    nextA["gen"] = phaseA(l + 1, abufs)
                hcnt = 0
                for ocp in range(4):
                    wo = wload(w_out_d[l, :, ocp * 256:(ocp + 1) * 256])
                    for oo in range(2):
                        oc = ocp * 2 + oo
                        for tc, (t0, n) in enumerate(TCL):
                            s = 0 if tc < 4 else 1
                            bi = (oc * 5 + tc) % 8
                            for k in range(8):
                                mm(bk(bi, 0, 128, 0, n), wo[:, k, oo * 128:(oo + 1) * 128], yT[:, k, t0:t0 + n], k == 0, k == 7)
                            stt("dve", xT[:, oc, t0:t0 + n], bk(bi, 0, 128, 0, n), gh_c(oc, s), xT[:, oc, t0:t0 + n], ALU.mult, ALU.add)
                            hcnt += 1
                            if hcnt % 3 == 0:
                                stepA(1)

        mark('final')
        A = Alloc(E0)
        sqb = [A.get([128, 512], BF16, "sqb%d" % i) for i in range(3)]
        msb = [A.get([128, 512], F32, "ms%d" % i) for i in range(2)]
        rsb = [A.get([128, 512], F32, "rs%d" % i) for i in range(2)]
        ob = [A.get([128, 512], F32, "ob%d" % i) for i in range(4)]
        oi = 0
        for tc, (t0, n) in enumerate(TCH[:4]):
            bi = tc % 4
            for k in range(8):
                sq = sqb[k % 3]
                act(sq[:, :n], xT[:, k, t0:t0 + n], AF.Square)
                mm(bk(bi, 0, 128, 0, n), onesb[:, 0, :], sq[:, :n], k == 0, k == 7)
            ms, rs = msb[tc % 2], rsb[tc % 2]
            rstd_from(bk(bi, 0, 128, 0, n), n, RMS_EPS, ms, rs)
            for k in range(8):
                o = ob[oi % 4]
                oi += 1
                stt("dve", o[:, :n], xT[:, k, t0:t0 + n], pv[:, PV_FINAL + k:PV_FINAL + k + 1], rs[:, :n], ALU.mult, ALU.mult)
                S.dma("sp", outT_d[k * 128:(k + 1) * 128, t0:t0 + n], o[:, :n])
        S.finish()
        stats_ = S.emit()
        stats_['marks'] = marks
    return nc, stats_


def _fm(v):
    v = np.asarray(v, np.float32)
    return np.ascontiguousarray(v.reshape(-1, 128).T)


def _swap_idx():
    idx = []
    for r in range(32):
        axis, half, f = r // 16, (r % 16) // 8, r % 8
        idx.append(axis * 16 + (1 - half) * 8 + f)
    return idx


def _consts():
    t = np.arange(NLAT)
    row = (t // 64).astype(np.float64)
    col = (t % 64).astype(np.float64)
    inv = 10000.0 ** (-np.arange(0, 16, 2, dtype=np.float64) / 16)
    rope = np.zeros((2, 128, NLAT), np.float32)
    for r in range(32):
        axis, half, f = r // 16, (r % 16) // 8, r % 8
        pos = row if axis == 0 else col
        ang = (pos.astype(np.float32) * np.float32(inv[f])).astype(np.float64)
        rope[0, 64 + r] = np.cos(ang)
        rope[1, 64 + r] = np.sin(ang) * (-1.0 if half == 0 else 1.0)
    n = np.arange(NLAT, dtype=np.int64)
    ang = 2 * np.pi * ((n[:, None] * n[None, :]) % NLAT) / NLAT
    dftL = np.stack([np.cos(ang), -np.sin(ang)]) / math.sqrt(NLAT)
    m = np.arange(NCTX, dtype=np.int64)
    angc = 2 * np.pi * ((m[:, None] * m[None, :]) % NCTX) / NCTX
    dftC = np.stack([np.cos(angc), -np.sin(angc)]) / math.sqrt(NCTX)
    j = np.arange(64)
    a64 = 2 * np.pi * ((j[:, None] * j[None, :]) % 64) / 64
    cs = np.zeros((128, 256), np.float64)
    for b in range(2):
        cs[b * 64:(b + 1) * 64, b * 64:(b + 1) * 64] = np.cos(a64) / 8
        cs[b * 64:(b + 1) * 64, 128 + b * 64:128 + (b + 1) * 64] = np.sin(a64) / 8
    bf = ml_dtypes.bfloat16
    return dict(ropeT=rope, dftL=dftL.astype(np.float32).astype(bf), dftC=dftC.astype(np.float32).astype(bf),
                cs128=cs.astype(np.float32).astype(bf), ident=np.eye(128, dtype=np.float32).astype(bf))


def prep_inputs(inp):
    g = lambda k: np.asarray(inp[k], np.float32)
    L = DEPTH
    x, c, ctx, c_ctx = g("x"), g("c"), g("ctx"), g("c_ctx")
    w_in, w_uq, w_ukv = g("w_in"), g("w_uq"), g("w_ukv")
    sw = _swap_idx()
    perm_q = []
    for h in range(4):
        perm_q += [h * 96 + d for d in range(64)] + [h * 96 + 64 + s for s in sw]
    cols_rb = list(range(320, 384)) + [384 + s for s in sw]
    nope = [h * 128 + d for h in range(4) for d in range(64)]
    vv = [h * 128 + 64 + d for h in range(4) for d in range(64)]
    shared = dict(
        w_ada=g("w_ada"), w_in=w_in, w_ropeB=np.ascontiguousarray(w_in[:, :, cols_rb]),
        w_uq=w_uq, w_uqs=np.ascontiguousarray(w_uq[:, :, perm_q]),
        w_ukv2=np.ascontiguousarray(w_ukv[:, :, nope + vv]),
        w_pw=g("w_pw"), w_fourier=g("w_fourier"),
        w_sT=np.ascontiguousarray(g("w_s").transpose(0, 3, 1, 2).reshape(L, 128, 512)),
        b_s=np.ascontiguousarray(g("b_s").reshape(L, 1, 512)),
        sgu_gb=np.ascontiguousarray(np.concatenate([
            np.broadcast_to(g("sgu_ln_g")[:, None, :], (L, 128, 256)),
            np.broadcast_to(g("sgu_ln_b")[:, None, :], (L, 128, 256))], axis=2)),
        w_out=g("w_out"),
    )
    shared.update(_consts())
    pvs = []
    for l in range(L):
        cw = np.concatenate([g("conv_w")[l][:, cc * 128:(cc + 1) * 128].T for cc in range(2)], axis=1)
        pvs += [np.repeat(_fm(g("norm_g")[l]), 2, axis=1), np.repeat(_fm(g("b_ada")[l]), 2, axis=1),
                _fm(g("q_norm_g")[l]), _fm(g("kv_norm_g")[l]), _fm(g("conv_b")[l]), _fm(g("conv_ln_g")[l]),
                _fm(g("conv_ln_b")[l]), _fm(g("b_pw")[l]), _fm(g("b_fourier")[l]), cw]
    pvs.append(_fm(g("final_g")))
    pv_common = np.concatenate(pvs, axis=1)
    assert pv_common.shape[1] == PV_C, pv_common.shape
    in_maps = []
    for b in range(8):
        cp_ = np.stack([_fm(c[b]), _fm(c_ctx)], axis=2).reshape(128, 16)
        m = dict(shared)
        m["xT"] = np.ascontiguousarray(np.concatenate([x[b].T, ctx[b].T], axis=1))
        m["pv"] = np.ascontiguousarray(np.concatenate([pv_common, cp_], axis=1).astype(np.float32))
        in_maps.append(m)
    return in_maps


_NC_CACHE = {}


def kernel(**inputs):
    in_maps = prep_inputs(inputs)
    if "nc" not in _NC_CACHE:
        _NC_CACHE["nc"] = build(DEPTH)[0]
    nc = _NC_CACHE["nc"]
    res = run_bass_kernel_spmd(nc, in_maps, core_ids=list(range(8)))
    out = np.stack([np.ascontiguousarray(r["outT"].T) for r in res.results], axis=0)
    return out.astype(np.float32)
```
